# Optimizing a Trainium2 kernel written in Bass

```python
import math
import jax, jax.numpy as jnp
from jax import lax
import numpy as np

D_MODEL = 1024
BATCH = 16
SEQ = 2048
DEPTH = 2

N_META = 16
BLOCK = 128
PAD_FRONT = BLOCK - N_META

ATT_HEADS = 8
ATT_KV_HEADS = 2
ATT_HEAD_DIM = 64
ATT_WIDTH = ATT_HEADS * ATT_HEAD_DIM
ATT_KV_WIDTH = ATT_KV_HEADS * ATT_HEAD_DIM
WINDOW = 128
N_BUCKETS = 32
MAX_EXACT = N_BUCKETS // 2
MAX_DISTANCE = 128

RET_HEADS = 4
RET_HEAD_DIM = 128
RET_WIDTH = RET_HEADS * RET_HEAD_DIM
ROT_BASE = 10000.0

CONV_WIDTH = 512
CONV_K = 3

N_BRANCH = 3
BRANCH_WIDTH = 512
SPLITS = (ATT_WIDTH, ATT_KV_WIDTH, ATT_KV_WIDTH, ATT_WIDTH,
          RET_WIDTH, RET_WIDTH, RET_WIDTH, RET_WIDTH,
          CONV_WIDTH, CONV_WIDTH, CONV_WIDTH, CONV_WIDTH,
          N_BRANCH * D_MODEL)
PROJ_WIDTH = 8448
RMS_EPS = 1e-6
GN_EPS = 1e-6
NEG_INF = -1e30

kernel_name = "hybrid_gated_swa_retention_shortconv"


def _split_points():
    return [int(v) for v in np.cumsum(SPLITS)[:-1]]


def rms_norm(x, g):
    xf = x.astype(jnp.float32)
    y = xf * lax.rsqrt(jnp.mean(xf * xf, axis=-1, keepdims=True) + RMS_EPS)
    return (y * g.astype(jnp.float32)).astype(x.dtype)


def t5_causal_bucket(dist):
    n = jnp.maximum(dist, 0)
    nf = jnp.maximum(n, 1).astype(jnp.float32)
    large = MAX_EXACT + (jnp.log(nf / MAX_EXACT) / math.log(MAX_DISTANCE / MAX_EXACT)
                         * (N_BUCKETS - MAX_EXACT)).astype(jnp.int32)
    large = jnp.minimum(large, N_BUCKETS - 1)
    return jnp.where(n < MAX_EXACT, n, large)


def sliding_window_attention(q, k, v, sinks, rel_bias, valid):
    B, Lp = q.shape[0], q.shape[1]
    nc = Lp // BLOCK
    G = ATT_HEADS // ATT_KV_HEADS
    qb = q.reshape(B, nc, BLOCK, ATT_KV_HEADS, G, ATT_HEAD_DIM)

    def band(t):
        tb = t.reshape((B, nc, BLOCK) + t.shape[2:])
        prev = jnp.concatenate([jnp.zeros_like(tb[:, :1]), tb[:, :-1]], axis=1)
        return jnp.concatenate([prev, tb], axis=2)

    kb, vb = band(k), band(v)
    vblk = valid.reshape(nc, BLOCK)
    vprev = jnp.concatenate([jnp.zeros_like(vblk[:1]), vblk[:-1]], axis=0)
    valid_band = jnp.concatenate([vprev, vblk], axis=1)

    r = jnp.arange(BLOCK)[:, None]
    c = jnp.arange(2 * BLOCK)[None, :]
    dist = BLOCK + r - c
    in_window = (dist >= 0) & (dist < WINDOW)
    bias = rel_bias[t5_causal_bucket(dist)]
    bias = bias.reshape(BLOCK, 2 * BLOCK, ATT_KV_HEADS, G).transpose(2, 3, 0, 1).astype(jnp.float32)
    mask = in_window[None] & valid_band[:, None, :]

    s = jnp.einsum('bnqhgd,bnkhd->bnhgqk', qb, kb).astype(jnp.float32) * (ATT_HEAD_DIM ** -0.5) + bias
    s = jnp.where(mask[None, :, None, None], s, NEG_INF)
    sink = sinks.astype(jnp.float32).reshape(ATT_KV_HEADS, G)[None, None, :, :, None, None]
    m = jnp.maximum(jnp.max(s, axis=-1, keepdims=True), sink)
    p = jnp.exp(s - m)
    denom = jnp.sum(p, axis=-1, keepdims=True) + jnp.exp(sink - m)
    p = (p / denom).astype(v.dtype)
    o = jnp.einsum('bnhgqk,bnkhd->bnqhgd', p, vb)
    return o.reshape(B, Lp, ATT_WIDTH)


def rotate(t, pos):
    half = t.shape[-1] // 2
    theta = 1.0 / (ROT_BASE ** jnp.linspace(0.0, 1.0, half, dtype=jnp.float32))
    ang = pos.astype(jnp.float32)[:, None] * theta[None, :]
    cos = jnp.cos(ang)[None, :, None, :]
    sin = jnp.sin(ang)[None, :, None, :]
    t1, t2 = t[..., :half].astype(jnp.float32), t[..., half:].astype(jnp.float32)
    return jnp.concatenate([t1 * cos - t2 * sin, t1 * sin + t2 * cos], axis=-1).astype(t.dtype)


def retention(q, k, v, valid, pos):
    B, Lp = q.shape[0], q.shape[1]
    nc = Lp // BLOCK
    q = rotate(q, pos)
    k = rotate(k, pos) * (RET_HEAD_DIM ** -0.5)
    k = k * valid[None, :, None, None].astype(k.dtype)
    log_gamma = jnp.log1p(-(2.0 ** (-5.0 - jnp.arange(RET_HEADS, dtype=jnp.float32))))
    i = jnp.arange(BLOCK, dtype=jnp.float32)
    diff = i[:, None] - i[None, :]
    decay = jnp.where(diff[None] >= 0, jnp.exp(diff[None] * log_gamma[:, None, None]), 0.0)
    zeta = jnp.exp((BLOCK - 1 - i)[None, :] * log_gamma[:, None])
    xi = jnp.exp((i + 1)[None, :] * log_gamma[:, None])
    gamma_chunk = jnp.exp(BLOCK * log_gamma)[None, :, None, None]

    shp = (B, nc, BLOCK, RET_HEADS, RET_HEAD_DIM)
    qc, kc, vc = q.reshape(shp), k.reshape(shp), v.reshape(shp)
    inner_s = jnp.einsum('bnihd,bnjhd->bnhij', qc, kc) * decay
    inner = jnp.einsum('bnhij,bnjhe->bnihe', inner_s, vc)
    chunk_kv = jnp.einsum('bnjhd,bnjhe,hj->nbhde', kc, vc, zeta)

    def step(state, kv):
        return gamma_chunk * state + kv, state

    _, prev_states = lax.scan(step, jnp.zeros_like(chunk_kv[0]), chunk_kv)
    cross = jnp.einsum('bnihd,nbhde,hi->bnihe', qc, prev_states, xi)
    o = (inner + cross).astype(jnp.float32)
    mu = jnp.mean(o, axis=-1, keepdims=True)
    var = jnp.mean(jnp.square(o - mu), axis=-1, keepdims=True)
    o = (o - mu) * lax.rsqrt(var + GN_EPS)
    return o.reshape(B, Lp, RET_WIDTH).astype(q.dtype)


def short_conv_mixer(b_gate, c_gate, x_in, conv_w, valid):
    u = c_gate * x_in * valid[None, :, None].astype(x_in.dtype)
    y = lax.conv_general_dilated(u, conv_w[:, None, :].astype(u.dtype), window_strides=(1,),
                                 padding=[(CONV_K - 1, 0)],
                                 dimension_numbers=('NWC', 'WIO', 'NWC'),
                                 feature_group_count=CONV_WIDTH)
    return b_gate * y


def hybrid_layer(x, valid, pos, rel_bias, g_pre, w_in, conv_w, sinks, w_branch, w_out, g_post):
    B, Lp, _ = x.shape
    h = rms_norm(x, g_pre)
    proj = h @ w_in
    (aq, ak, av, ag, rq, rk, rv, rg, cb, cc, cx, cg, merge) = jnp.split(proj, _split_points(), axis=-1)

    ya = sliding_window_attention(aq.reshape(B, Lp, ATT_HEADS, ATT_HEAD_DIM),
                                  ak.reshape(B, Lp, ATT_KV_HEADS, ATT_HEAD_DIM),
                                  av.reshape(B, Lp, ATT_KV_HEADS, ATT_HEAD_DIM),
                                  sinks, rel_bias, valid) * jax.nn.silu(ag)
    yr = retention(rq.reshape(B, Lp, RET_HEADS, RET_HEAD_DIM),
                   rk.reshape(B, Lp, RET_HEADS, RET_HEAD_DIM),
                   rv.reshape(B, Lp, RET_HEADS, RET_HEAD_DIM), valid, pos) * jax.nn.silu(rg)
    yc = short_conv_mixer(cb, cc, cx, conv_w, valid) * jax.nn.silu(cg)

    branches = jnp.stack([ya, yr, yc], axis=2)
    branch_out = jnp.einsum('blgc,gcd->blgd', branches, w_branch)
    gates = jax.nn.sigmoid(merge.reshape(B, Lp, N_BRANCH, D_MODEL))
    mixed = jnp.sum(gates * branch_out, axis=2) @ w_out
    return x + rms_norm(mixed, g_post).astype(x.dtype)


def setup_inputs(seed: int = 0) -> dict:
    key = jax.random.key(seed)
    ks = jax.random.split(key, 10)
    f32 = jnp.float32
    x = jax.random.normal(ks[0], (BATCH, SEQ, D_MODEL), f32)
    meta_tokens = jax.random.normal(ks[1], (N_META, D_MODEL), f32)
    rel_bias = 0.1 * jax.random.normal(ks[2], (N_BUCKETS, ATT_HEADS), f32)
    norm_pre = 1.0 + 0.01 * jax.random.normal(ks[3], (DEPTH, D_MODEL), f32)
    w_in = jax.random.normal(ks[4], (DEPTH, D_MODEL, PROJ_WIDTH), f32) * (D_MODEL ** -0.5)
    conv_w = jax.random.normal(ks[5], (DEPTH, CONV_K, CONV_WIDTH), f32) * (CONV_K ** -0.5)
    attn_sinks = 0.5 * jax.random.normal(ks[6], (DEPTH, ATT_HEADS), f32)
    w_branch = jax.random.normal(ks[7], (DEPTH, N_BRANCH, BRANCH_WIDTH, D_MODEL), f32) * (BRANCH_WIDTH ** -0.5)
    w_out = jax.random.normal(ks[8], (DEPTH, D_MODEL, D_MODEL), f32) * (D_MODEL ** -0.5)
    norm_post = 1.0 + 0.01 * jax.random.normal(ks[9], (DEPTH, D_MODEL), f32)
    return {"x": x, "meta_tokens": meta_tokens, "rel_bias": rel_bias, "norm_pre": norm_pre,
            "w_in": w_in, "conv_w": conv_w, "attn_sinks": attn_sinks, "w_branch": w_branch,
            "w_out": w_out, "norm_post": norm_post}


def reference(x, meta_tokens, rel_bias, norm_pre, w_in, conv_w, attn_sinks, w_branch, w_out, norm_post):
    B, S, _ = x.shape
    pad = jnp.zeros((B, PAD_FRONT, D_MODEL), x.dtype)
    meta = jnp.broadcast_to(meta_tokens[None].astype(x.dtype), (B, N_META, D_MODEL))
    h = jnp.concatenate([pad, meta, x], axis=1)
    idx = jnp.arange(PAD_FRONT + N_META + S)
    valid = idx >= PAD_FRONT
    pos = idx - PAD_FRONT
    for l in range(DEPTH):
        h = hybrid_layer(h, valid, pos, rel_bias, norm_pre[l], w_in[l], conv_w[l],
                         attn_sinks[l], w_branch[l], w_out[l], norm_post[l])
    return h[:, PAD_FRONT + N_META:]
```

```python
import math
from contextlib import ExitStack

import numpy as np
import concourse.bass as bass
import concourse.mybir as mybir
from concourse.bass_utils import run_bass_kernel_spmd

F32 = mybir.dt.float32
BF16 = mybir.dt.bfloat16
I32 = mybir.dt.int32
ALU = mybir.AluOpType
AF = mybir.ActivationFunctionType
AX = mybir.AxisListType

D = 1024
PW = 8448
NEG = -30000.0
RMS_EPS = 1e-6
GN_EPS = 1e-6
N_CORES = 8

WIN_TILES = {
    "aq": (0, 512), "akv": (512, 768), "ag": (768, 1280),
    "rq": (1280, 1792), "rk": (1792, 2304), "rv": (2304, 2816), "rg": (2816, 3328),
    "cb": (3328, 3840), "cc": (3840, 4352), "cx": (4352, 4864), "cg": (4864, 5376),
}
for _g in range(3):
    for _h in range(2):
        WIN_TILES["m%d%d" % (_g, _h)] = (5376 + _g * 1024 + _h * 512, 5376 + _g * 1024 + (_h + 1) * 512)


class Buf:
    __slots__ = ("name", "writers", "readers")

    def __init__(self, name=""):
        self.name = name
        self.writers = {}
        self.readers = {}


class DmaSem:
    def __init__(self, sem, group=False):
        self.sem = sem
        self.count = 0
        self.group = group


class Op:
    __slots__ = ("eng", "fn", "waits", "signal", "dma", "order", "sigval")


ENGS = ("pe", "act", "dve", "pool", "sp")


class Sched:
    def __init__(self):
        self.q = {e: [] for e in ENGS}
        self.seen = {e: {} for e in ENGS}
        self.ncomp = {e: 0 for e in ENGS}

    def add(self, eng, fn, reads=(), writes=(), dma=None):
        op = Op()
        op.eng = eng
        op.fn = fn
        op.waits = []
        op.signal = False
        op.dma = dma
        op.sigval = None
        if dma is not None:
            dma.count += 16
            op.order = dma.count
        else:
            self.ncomp[eng] += 1
            op.order = self.ncomp[eng]
        deps = []
        for b in reads:
            deps.extend(b.writers.values())
        for b in writes:
            deps.extend(b.writers.values())
            deps.extend(b.readers.values())
        seen = self.seen[eng]
        for d in deps:
            if d is op:
                continue
            if d.dma is None and d.eng == "pe" and eng == "pe" and dma is None:
                continue
            if d.dma is not None and d.dma.group and d.dma is dma:
                continue
            key = id(d.dma) if d.dma is not None else d.eng
            order = (1 << 60) if (d.dma is not None and d.dma.group) else d.order
            if seen.get(key, -1) >= order:
                continue
            seen[key] = order
            op.waits.append(d)
            d.signal = True
        mykey = id(dma) if dma is not None else eng
        for b in writes:
            if b.readers:
                b.writers = {mykey: op}
                b.readers = {}
            else:
                b.writers[mykey] = op
        for b in reads:
            b.readers[mykey] = op
        self.q[eng].append(op)
        return op

    def emit(self, block, sems):
        for e in ENGS:
            c = 0
            for op in self.q[e]:
                if op.dma is None and op.signal:
                    c += 1
                    op.sigval = c
        decos = {"pe": block.tensor, "act": block.scalar, "dve": block.vector,
                 "pool": block.gpsimd, "sp": block.sync}
        for eng in ENGS:
            ops = self.q[eng]

            def body(e, ops=ops, eng=eng):
                for op in ops:
                    for d in op.waits:
                        if d.dma is not None:
                            e.wait_ge(d.dma.sem, d.dma.count if d.dma.group else d.order)
                        else:
                            e.wait_ge(sems[d.eng], d.sigval)
                    ins = op.fn(e)
                    if op.dma is not None:
                        ins.then_inc(op.dma.sem, 16)
                    elif op.signal:
                        ins.then_inc(sems[eng], 1)

            decos[eng](body)


class TB:
    __slots__ = ("t", "b")

    def __init__(self, t, name):
        self.t = t
        self.b = Buf(name)


def t5_bucket_np(d):
    d = np.asarray(d)
    n = np.maximum(d, 0)
    nf = np.maximum(n, 1).astype(np.float32)
    large = 16 + (np.log(nf / np.float32(16)) / np.float32(math.log(128 / 16)) * np.float32(16)).astype(np.int32)
    large = np.minimum(large, 31)
    return np.where(n < 16, n, large)


def host_consts(nblk):
    half = 64
    lin = np.linspace(0.0, 1.0, half, dtype=np.float32)
    theta = (np.float32(1.0) / np.power(np.float32(10000.0), lin)).astype(np.float32)
    pos = (np.arange(nblk * 128) - 112).astype(np.float32)
    ang = (pos[:, None] * theta[None, :]).astype(np.float32)
    cos = np.cos(ang).astype(np.float32)
    sin = np.sin(ang).astype(np.float32)
    rot = np.concatenate([cos, cos, -sin, sin], axis=1).reshape(nblk, 128, 256).astype(np.float32)

    lg = np.log1p(-(2.0 ** (-5.0 - np.arange(4, dtype=np.float64))))
    i = np.arange(128, dtype=np.float64)
    gq = np.exp(i[None, :] * lg[:, None])
    gk = np.exp(-i[None, :] * lg[:, None]) * (128.0 ** -0.5)
    g128 = np.exp(128.0 * lg)
    ncf = 1552
    cf = np.zeros((128, ncf), np.float32)
    jj = np.arange(128)[:, None]
    ii = np.arange(128)[None, :]
    mask = (ii >= jj).astype(np.float32)
    cf[:, 0:512] = np.tile(mask[:, None, :], (1, 4, 1)).reshape(128, 512)
    cf[:, 512:1024] = gq.reshape(1, 512)
    cf[:, 1024:1536] = gk.reshape(1, 512)
    cf[:, 1536:1540] = gk.T
    cf[:, 1540:1544] = g128[None, :]
    cf[:, 1544] = (np.arange(128) >= 112).astype(np.float32)
    cf[:, 1545] = 1.0
    cb = np.zeros((128, 5, 128), np.float32)
    tp = np.arange(128)[:, None]
    t = np.arange(128)[None, :]
    cb[:, 0, :] = (tp == t)
    cb[:, 1, :] = (tp == t - 2)
    cb[:, 2, :] = (tp == t - 1)
    cb[:, 3, :] = (tp == 128 + t - 2)
    cb[:, 4, :] = (tp == 128 + t - 1)
    return rot, cf, cb.reshape(128, 640)


def bucket_runs():
    bk = t5_bucket_np(np.arange(128))
    runs = []
    d = 16
    while d < 128:
        e = d
        while e + 1 < 128 and bk[e + 1] == bk[d]:
            e += 1
        runs.append((int(bk[d]), d, e))
        d = e + 1
    return runs


def build_program(nseq=2, nblk=17, depth=2, NB=4, NSLOT=3, dbg=None, PT_ALL=False, DEFER=True, DL=(6, 3, 6, 8, 6), NEWTON=2):
    nc = bass.Bass("TRN2", target_bir_lowering=False)
    S = Sched()
    SL = (nblk - 1) * 128
    P = 128

    def din(name, shape):
        return nc.dram_tensor(name, list(shape), F32, kind="ExternalInput").ap()

    x_d = din("x", [nseq, SL, D])
    meta_d = din("meta_tokens", [16, D])
    relb_d = din("rel_bias", [32, 8])
    npre_d = din("norm_pre", [depth, D])
    win_d = din("w_in", [depth, D, PW])
    convw_d = din("conv_w", [depth, 3, 512])
    sinks_d = din("attn_sinks", [depth, 8])
    wbr_d = din("w_branch", [depth, 3, 512, D])
    wout_d = din("w_out", [depth, D, D])
    npost_d = din("norm_post", [depth, D])
    rot_d = din("c_rot", [nblk, 128, 256])
    cf_d = din("c_f32", [128, 1552])
    cb_d = din("c_mat", [128, 640])
    out_d = nc.dram_tensor("out", [nseq, SL, D], F32, kind="ExternalOutput").ap()
    xs_h = [nc.dram_tensor("xs%d" % l, [nseq, nblk * 128, D], F32) for l in range(max(depth - 1, 1))]
    xs_d = [h.ap() for h in xs_h]
    xs_b = [[Buf("xs%d_%d" % (l, i)) for i in range(nseq * nblk)] for l in range(len(xs_h))]
    G_h = nc.dram_tensor("Gscr", [16, 128, 256], F32)
    G_b = Buf("G")

    es = ExitStack()
    with es:
        cnt = [0]

        def sb(shape, dt, name=None):
            cnt[0] += 1
            nm = "%s_%d" % (name or "t", cnt[0])
            return TB(es.enter_context(nc.sbuf_tensor(nm, list(shape), dt)), nm)

        def sem(name):
            cnt[0] += 1
            return es.enter_context(nc.semaphore("%s_%d" % (name, cnt[0])))

        def dsem(name, group=False):
            return DmaSem(sem(name), group)

        class Rot:
            def __init__(self, n, shape, dt, name, with_sem=False):
                self.items = [sb(shape, dt, name) for _ in range(n)]
                self.sems = [dsem("d_" + name) for _ in range(n)] if with_sem else None
                self.i = -1

            def next(self):
                self.i = (self.i + 1) % len(self.items)
                return self.items[self.i]

            def next_with_sem(self):
                self.i = (self.i + 1) % len(self.items)
                return self.items[self.i], self.sems[self.i]

        sems = {e: sem("s_" + e) for e in ENGS}

        psum = []
        for k in range(8):
            t = es.enter_context(nc.psum_tensor("ps%d" % k, [128, 512], F32))
            psum.append(TB(t, "ps%d" % k))
        ps_i = [-1]

        def PS():
            ps_i[0] = (ps_i[0] + 1) % 8
            return psum[ps_i[0]]

        cf = sb([128, 1552], F32, "cf")
        cmat = sb([128, 5, 128], BF16, "cmat")
        d_init = dsem("d_init", group=True)
        d_initp = dsem("d_initp", group=True)
        S.add("sp", lambda e: e.dma_start(out=cf.t[:], in_=cf_d), writes=[cf.b], dma=d_init)
        S.add("pool", lambda e: e.dma_start(out=cmat.t[:], in_=cb_d.rearrange("p (a b) -> p a b", a=5)),
              writes=[cmat.b], dma=d_initp)
        ident = cmat.t[:, 0, :]
        maskT = cf.t[:, 0:512]
        gqT = cf.t[:, 512:1024]
        gkT = cf.t[:, 1024:1536]
        gk_tok = cf.t[:, 1536:1540]
        g128 = cf.t[:, 1540:1544]

        esink = sb([128, depth * 8], F32, "esink")
        S.add("sp", lambda e: e.dma_start(
            out=esink.t[:], in_=sinks_d.rearrange("l h -> (l h)").rearrange("(o n) -> o n", o=1).broadcast_to([128, depth * 8])),
            writes=[esink.b], dma=d_init)
        S.add("act", lambda e: e.activation(out=esink.t[:], in_=esink.t[:], func=AF.Exp), reads=[esink.b], writes=[esink.b])

        rbT = sb([8, 32], F32, "rbT")
        Lt = sb([8, 384], F32, "Lt")
        biasT = [[sb([128, 512], BF16, "biasT") for _ in range(2)] for _ in range(2)]
        bias_stage = sb([128, 128], F32, "bstage")
        S.add("sp", lambda e: e.dma_start(out=rbT.t[:], in_=relb_d.rearrange("b h -> h b")), writes=[rbT.b], dma=d_init)
        S.add("dve", lambda e: e.memset(Lt.t[:], NEG), writes=[Lt.b])
        S.add("dve", lambda e: e.tensor_copy(out=Lt.t[:, 255:271], in_=rbT.t[:, 0:16]), reads=[rbT.b], writes=[Lt.b])
        S.add("dve", lambda e: e.tensor_copy(out=Lt.t[:, 0:15], in_=rbT.t[:, 1:16]), reads=[rbT.b], writes=[Lt.b])
        for (bk, dlo, dhi) in bucket_runs():
            n = dhi - dlo + 1
            S.add("dve", lambda e, bk=bk, dlo=dlo, n=n: e.tensor_copy(
                out=Lt.t[:, 255 + dlo:255 + dlo + n], in_=rbT.t[:, bk:bk + 1].broadcast_to([8, n])),
                reads=[rbT.b], writes=[Lt.b])
            S.add("dve", lambda e, bk=bk, dlo=dlo, n=n: e.tensor_copy(
                out=Lt.t[:, dlo - 1:dlo - 1 + n], in_=rbT.t[:, bk:bk + 1].broadcast_to([8, n])),
                reads=[rbT.b], writes=[Lt.b])
        d_G = dsem("d_G", group=True)
        d_bias = dsem("d_bias")
        for kb in range(2):
            S.add("sp", lambda e, kb=kb: e.dma_start(
                out=bass.AP(G_h, kb * 128 * 256, [[2 * 128 * 256, 8], [256, 128], [1, 256]]),
                in_=bass.AP(Lt.t, kb * 128, [[384, 8], [0, 128], [1, 256]])),
                reads=[Lt.b], writes=[G_b], dma=d_G)
        for h in range(8):
            kvh, g = h // 4, h % 4
            cc, hh = g // 2, g % 2
            for kb in range(2):
                dst = biasT[kb][hh]
                S.add("sp", lambda e, h=h, kb=kb: e.dma_start(
                    out=bias_stage.t[:],
                    in_=bass.AP(G_h, (h * 2 + kb) * 128 * 256 + 127, [[255, 128], [1, 128]])),
                    reads=[G_b], writes=[bias_stage.b], dma=d_bias)
                S.add("dve", lambda e, dst=dst, kvh=kvh, cc=cc: e.tensor_copy(
                    out=dst.t[:, (kvh * 2 + cc) * 128:(kvh * 2 + cc + 1) * 128], in_=bias_stage.t[:]),
                    reads=[bias_stage.b], writes=[dst.b])

        gpre = sb([128, D], F32, "gpre")
        gpost = sb([128, D], F32, "gpost")
        wconv = sb([128, 3, 512], F32, "wconv")
        d_lc = [dsem("d_lc") for _ in range(3)]

        def layer_consts(l):
            S.add("sp", lambda e: e.dma_start(out=gpre.t[:], in_=npre_d[l:l + 1, :].broadcast_to([128, D])),
                  writes=[gpre.b], dma=d_lc[0])
            S.add("sp", lambda e: e.dma_start(out=gpost.t[:], in_=npost_d[l:l + 1, :].broadcast_to([128, D])),
                  writes=[gpost.b], dma=d_lc[1])
            S.add("sp", lambda e: e.dma_start(
                out=wconv.t[:].rearrange("p a b -> p (a b)"),
                in_=convw_d[l].rearrange("k c -> (k c)").rearrange("(o n) -> o n", o=1).broadcast_to([128, 1536])),
                writes=[wconv.b], dma=d_lc[2])
            S.add("dve", lambda e: e.tensor_scalar(out=gpre.t[:], in0=gpre.t[:], scalar1=32.0, scalar2=None, op0=ALU.mult),
                  reads=[gpre.b], writes=[gpre.b])
            S.add("dve", lambda e: e.tensor_scalar(out=gpost.t[:], in0=gpost.t[:], scalar1=32.0, scalar2=None, op0=ALU.mult),
                  reads=[gpost.b], writes=[gpost.b])

        ring = [sb([128, 8, 512], BF16, "ring") for _ in range(NSLOT)]
        ring_sem = [dsem("d_ring") for _ in range(NSLOT)]
        wtiles = []
        w_issued = [0]

        def w_issue_upto(k):
            while w_issued[0] <= k and w_issued[0] < len(wtiles):
                i = w_issued[0]
                src, nkc, ncols = wtiles[i]
                slot = i % NSLOT
                S.add("pool", lambda e, src=src, nkc=nkc, ncols=ncols, slot=slot: e.dma_start(
                    out=ring[slot].t[:, 0:nkc, 0:ncols], in_=src),
                    writes=[ring[slot].b], dma=ring_sem[slot])
                w_issued[0] += 1

        w_next = [0]

        def w_take():
            k = w_next[0]
            w_next[0] += 1
            w_issue_upto(k + NSLOT - 1)
            return ring[k % NSLOT]

        NBP = NB + 1
        xt_pool = Rot(2, [128, D], F32, "xt", with_sem=True)
        rt_tiles = [sb([128, 256], F32, "rt") for _ in range(NB)]
        rt_sems = [dsem("d_rt") for _ in range(NB)]
        ccf = [sb([128, 512], BF16, "ccf") for _ in range(NB)]
        xr_pool = Rot(2, [128, D], F32, "xr", with_sem=True)
        out_bs = [Buf("out0"), Buf("out1")]
        st_pool = Rot(4, [128, 16], F32, "st")
        h_pool = Rot(2, [128, D], BF16, "h")
        hT2 = [[sb([128, 8, 128], BF16, "hT") for _ in range(NB)] for _ in range(2)]
        brT = [sb([128, 12, 128], BF16, "brT") for _ in range(NB)]
        brT_a = [Buf("brTa") for _ in range(NB)]
        brT_r = [Buf("brTr") for _ in range(NB)]
        brT_c = [Buf("brTc") for _ in range(NB)]
        arena_t = [es.enter_context(nc.sbuf_tensor("arena%d" % j, [128, 7, 512], BF16)) for j in range(NB)]

        class _V:
            __slots__ = ("t", "b")

            def __init__(self, t, b):
                self.t = t
                self.b = b
        arena = [[_V(arena_t[j][:, k, :], Buf("ar%d_%d" % (j, k))) for k in range(7)] for j in range(NB)]
        mpT = [_V(arena_t[j][:, 4:6, :].rearrange("p a (c d) -> p (a c) d", d=128), arena[j][4].b) for j in range(NB)]
        mp = [_V(arena_t[j][:, 2:4, :].rearrange("p a c -> p (a c)"), arena[j][2].b) for j in range(NB)]
        kTd = [sb([128, 2, 128], BF16, "kTd") for _ in range(NBP)]
        Vx = [sb([128, 2, 65], BF16, "Vx") for _ in range(NBP)]
        ust = [sb([128, 512], BF16, "ust") for _ in range(NB)]
        uwt2 = [sb([128, 3, 512], BF16, "uwt") for _ in range(2)]
        tm_pool = Rot(NB + 1, [128, 512], BF16, "tm")
        accmx = [sb([128, D], F32, "accmx") for _ in range(NB)]
        ssq = [sb([128, 4], F32, "ssq") for _ in range(NB)]
        f512 = Rot(5, [128, 512], F32, "f512")
        b512 = Rot(6, [128, 512], BF16, "b512")
        ybr = Rot(NB + 1, [128, 512], BF16, "ybr")
        pT_pool = Rot(8, [128, 512], BF16, "pT")
        Zst = sb([128, 512], F32, "Z")
        Zb = [sb([128, 512], BF16, "Zb") for _ in range(NBP)]
        small = Rot(6, [128, 16], F32, "small")

        def dump(name, tb, ap, shape):
            o = nc.dram_tensor("dbg_" + name, list(shape), F32, kind="ExternalOutput").ap()
            stg = sb(shape, F32, "dbgstg")
            S.add("dve", lambda e: e.tensor_copy(out=stg.t[:], in_=ap), reads=[tb.b], writes=[stg.b])
            S.add("sp", lambda e: e.dma_start(out=o, in_=stg.t[:]), reads=[stg.b], writes=[dbg_b], dma=d_dbg)

        d_dbg = dsem("d_dbg", group=True)
        dbg_b = Buf("dbg")

        pending = []
        depth_ = [0]
        _raw_add = S.add

        def _emit_items(items):
            depth_[0] += 1
            for it in items:
                it["emit"]()
            depth_[0] -= 1

        def _flush(n):
            items = [pending.pop(0) for _ in range(min(n, len(pending)))]
            _emit_items(items)

        def _flush_sel(hits):
            need = set(hits)
            changed = True
            while changed:
                changed = False
                for i in sorted(need):
                    bi = pending[i]
                    allb = bi["src"] | bi["dst"]
                    for a in range(i):
                        if a in need:
                            continue
                        ai = pending[a]
                        if (ai["dst"] & allb) or (ai["src"] & bi["dst"]):
                            need.add(a)
                            changed = True
            items = [pending[i] for i in sorted(need)]
            for i in sorted(need, reverse=True):
                pending.pop(i)
            _emit_items(items)

        def _add(eng, fn, reads=(), writes=(), dma=None):
            if pending:
                wids = set(id(b) for b in writes)
                ids = wids | set(id(b) for b in reads)
                hits = [i for i, it in enumerate(pending) if (it["dst"] & ids) or (it["src"] & wids)]
                if hits:
                    _flush_sel(hits)
                if eng == "pe" and depth_[0] == 0:
                    for it in pending:
                        it["age"] += 1
                    n = 0
                    while n < len(pending) and pending[n]["age"] > pending[n]["delay"]:
                        n += 1
                    if n:
                        _flush(n)
            return _raw_add(eng, fn, reads, writes, dma)
        S.add = _add

        def defer(emit, src, dst, delay):
            if delay <= 0 or not DEFER:
                emit()
                return
            pending.append({"emit": emit, "src": set(id(b) for b in src), "dst": set(id(b) for b in dst), "age": 0, "delay": delay})

        def rsqrt_small(dst_ap, src_ap, tmp, ncol, srcb):
            ti = tmp.t.bitcast(I32)
            a = tmp.t[:, 0:ncol]
            b = tmp.t[:, ncol:2 * ncol]
            c = tmp.t[:, 2 * ncol:3 * ncol]
            rw = list(srcb) + [tmp.b]
            S.add("dve", lambda e: e.tensor_single_scalar(out=ti[:, 0:ncol], in_=src_ap.bitcast(I32), scalar=1,
                                                          op=ALU.arith_shift_right), reads=rw, writes=[tmp.b])
            S.add("dve", lambda e: e.tensor_scalar(out=dst_ap.bitcast(I32), in0=ti[:, 0:ncol], scalar1=-1.0,
                                                   scalar2=1597463007.0, op0=ALU.mult, op1=ALU.add), reads=rw, writes=rw)
            for _ in range(NEWTON):
                if ncol == 1:
                    S.add("dve", lambda e: e.scalar_tensor_tensor(out=b, in0=dst_ap, scalar=src_ap, in1=dst_ap, op0=ALU.mult, op1=ALU.mult),
                          reads=rw, writes=rw)
                else:
                    S.add("dve", lambda e: e.tensor_tensor(out=a, in0=dst_ap, in1=dst_ap, op=ALU.mult), reads=rw, writes=rw)
                    S.add("dve", lambda e: e.tensor_tensor(out=b, in0=a, in1=src_ap, op=ALU.mult), reads=rw, writes=rw)
                S.add("dve", lambda e: e.tensor_scalar(out=c, in0=b, scalar1=-0.5, scalar2=1.5, op0=ALU.mult, op1=ALU.add),
                      reads=rw, writes=rw)
                S.add("dve", lambda e: e.tensor_tensor(out=dst_ap, in0=dst_ap, in1=c, op=ALU.mult), reads=rw, writes=rw)

        def transposes(src_tb, src_aps, dst_tb, dst_ap, evac_eng="act", evac_mul=None, src_extra=(), dst_extra=(), delay=0):
            n = len(src_aps)

            def emit():
                ps = PS()
                psb = ps.t.bitcast(BF16)

                def fn(e):
                    for k, ap in enumerate(src_aps):
                        ins = e.transpose(out=psb[:, k * 128:(k + 1) * 128], in_=ap, identity=ident)
                    return ins
                S.add("pe", fn, reads=[src_tb.b, cmat.b] + list(src_extra), writes=[ps.b])
                src = psb[:, 0:n * 128]
                if evac_mul is not None:
                    S.add("dve", lambda e: e.tensor_tensor(out=dst_ap, in0=src, in1=evac_mul, op=ALU.mult),
                          reads=[ps.b, cf.b], writes=[dst_tb.b] + list(dst_extra))
                elif evac_eng == "act":
                    S.add("act", lambda e: e.copy(out=dst_ap, in_=src), reads=[ps.b], writes=[dst_tb.b] + list(dst_extra))
                else:
                    S.add("dve", lambda e: e.tensor_copy(out=dst_ap, in_=src), reads=[ps.b], writes=[dst_tb.b] + list(dst_extra))
            defer(emit, [src_tb.b] + list(src_extra), [dst_tb.b] + list(dst_extra), delay)

        def gate_evac(ps, dst):
            t = f512.next()
            S.add("act", lambda e: e.activation(out=t.t[:], in_=ps.t[:, :], func=AF.Tanh, scale=0.5), reads=[ps.b], writes=[t.b])
            S.add("dve", lambda e: e.scalar_tensor_tensor(out=dst.t, in0=t.t[:], scalar=1.0, in1=ps.t[:, :],
                                                          op0=ALU.add, op1=ALU.mult), reads=[t.b, ps.b], writes=[dst.b])

        def proj(wt, lhs, ncols, nkc=8, extra=()):
            ps = PS()
            lhs_aps = [lhs.t[:, kc, :] for kc in range(nkc)]

            def fn(e):
                for kc in range(nkc):
                    ins = e.matmul(ps.t[:, 0:ncols], lhsT=lhs_aps[kc], rhs=wt.t[:, kc, 0:ncols],
                                   start=(kc == 0), stop=(kc == nkc - 1))
                return ins
            S.add("pe", fn, reads=[lhs.b, wt.b] + list(extra), writes=[ps.b])
            return ps

        def win_tile(l, nm):
            c0, c1 = WIN_TILES[nm]
            return (win_d[l, :, c0:c1].rearrange("(kc p) n -> p kc n", p=128), 8, c1 - c0)

        def wb_tile(l, g, half):
            return (wbr_d[l, g, :, half * 512:(half + 1) * 512].rearrange("(kc p) n -> p kc n", p=128), 4, 512)

        def wo_tile(l, half):
            return (wout_d[l, :, half * 512:(half + 1) * 512].rearrange("(kc p) n -> p kc n", p=128), 8, 512)

        blocks = [(s, n) for s in range(nseq) for n in range(nblk)]
        sbs = [list(range(i, min(i + NB, len(blocks)))) for i in range(0, len(blocks), NB)]
        NSB = len(sbs)
        P1 = ["aq", "akv", "ag"]
        P2 = ["rq", "rk", "rv", "rg"]
        P3 = ["cc", "cx", "cg", "cb"]
        lc_done = set()

        def need_lc(kind, l):
            if (kind, l) in lc_done:
                return
            lc_done.add((kind, l))
            if kind == "pre":
                S.add("sp", lambda e: e.dma_start(out=gpre.t[:], in_=npre_d[l:l + 1, :].broadcast_to([128, D])),
                      writes=[gpre.b], dma=d_lc[0])
                S.add("dve", lambda e: e.tensor_scalar(out=gpre.t[:], in0=gpre.t[:], scalar1=32.0, scalar2=None, op0=ALU.mult),
                      reads=[gpre.b], writes=[gpre.b])
            elif kind == "post":
                S.add("sp", lambda e: e.dma_start(out=gpost.t[:], in_=npost_d[l:l + 1, :].broadcast_to([128, D])),
                      writes=[gpost.b], dma=d_lc[1])
                S.add("dve", lambda e: e.tensor_scalar(out=gpost.t[:], in0=gpost.t[:], scalar1=32.0, scalar2=None, op0=ALU.mult),
                      reads=[gpost.b], writes=[gpost.b])
            else:
                S.add("sp", lambda e: e.dma_start(
                    out=wconv.t[:].rearrange("p a b -> p (a b)"),
                    in_=convw_d[l].rearrange("k c -> (k c)").rearrange("(o n) -> o n", o=1).broadcast_to([128, 1536])),
                    writes=[wconv.b], dma=d_lc[2])

        def hTof(K, j):
            return hT2[K % 2][j]

        def act_blocks(l, k):
            last = (l == depth - 1)
            return [(j, gb) for j, gb in enumerate(sbs[k]) if not (last and blocks[gb][1] == 0)]

        def load_x(dst, sem_, l, gb):
            s, n = blocks[gb]
            if l == 0:
                if n == 0:
                    S.add("dve", lambda e: e.memset(dst.t[:], 0.0), writes=[dst.b])
                    S.add("sp", lambda e: e.dma_start(out=dst.t[112:128, :], in_=meta_d), writes=[dst.b], dma=sem_)
                else:
                    S.add("sp", lambda e: e.dma_start(out=dst.t[:], in_=x_d[s, (n - 1) * 128:n * 128, :]), writes=[dst.b], dma=sem_)
            else:
                assert xs_b[l - 1][gb].writers, "schedule error: layer input read before the previous layer stored it"
                S.add("sp", lambda e: e.dma_start(out=dst.t[:], in_=xs_d[l - 1][s, n * 128:(n + 1) * 128, :]),
                      reads=[xs_b[l - 1][gb]], writes=[dst.b], dma=sem_)

        n_x = {}

        def seg_NL(l, k, j):
            if j >= len(sbs[k]):
                return
            gb = sbs[k][j]
            xt, xsem = xt_pool.next_with_sem()
            n_x[(l, k, j)] = xt
            load_x(xt, xsem, l, gb)

        def seg_NC(l, k, j):
            if j >= len(sbs[k]):
                return
            K = l * NSB + k
            need_lc("pre", l)
            gb = sbs[k][j]
            s_, n = blocks[gb]
            xt = n_x.pop((l, k, j))
            rt, rsem = rt_tiles[j], rt_sems[j]
            S.add("sp", lambda e: e.dma_start(out=rt.t[:], in_=rot_d[n]), writes=[rt.b], dma=rsem)
            st = st_pool.next()
            hh_ = h_pool.next()
            S.add("act", lambda e: e.activation(out=hh_.t[:], in_=xt.t[:], func=AF.Square, accum_out=st.t[:, 0:1]),
                  reads=[xt.b], writes=[hh_.b, st.b])
            S.add("dve", lambda e: e.tensor_scalar(out=st.t[:, 1:2], in0=st.t[:, 0:1], scalar1=1024.0 * RMS_EPS,
                                                   scalar2=None, op0=ALU.add), reads=[st.b], writes=[st.b])
            rsqrt_small(st.t[:, 2:3], st.t[:, 1:2], small.next(), 1, [st.b])
            S.add("dve", lambda e: e.scalar_tensor_tensor(
                out=hh_.t[:], in0=xt.t[:], scalar=st.t[:, 2:3], in1=gpre.t[:], op0=ALU.mult, op1=ALU.mult),
                reads=[xt.b, st.b, gpre.b], writes=[hh_.b])
            ht = hTof(K, j)
            transposes(hh_, [hh_.t[:, c * 128:(c + 1) * 128] for c in range(8)], ht, ht.t[:].rearrange("p a b -> p (a b)"), delay=DL[0])

        def seg_N(l, k):
            for j in range(len(sbs[k])):
                seg_NL(l, k, j)
                seg_NC(l, k, j)

        def seg_A1(l, k):
            K = l * NSB + k
            for nm in P1:
                wt = w_take()
                for j, gb in enumerate(sbs[k]):
                    s, n = blocks[gb]
                    c0, c1 = WIN_TILES[nm]
                    ps = proj(wt, hTof(K, j), c1 - c0)
                    if nm == "aq":
                        q_sb = b512.next()
                        S.add("act", lambda e, ps=ps, q_sb=q_sb: e.activation(out=q_sb.t[:], in_=ps.t[:, :], func=AF.Copy, scale=0.125),
                              reads=[ps.b], writes=[q_sb.b])
                        transposes(q_sb, [q_sb.t[:, c * 128:(c + 1) * 128] for c in range(4)], arena[j][0], arena[j][0].t, evac_eng="dve", delay=DL[1])
                    elif nm == "akv":
                        kd = b512.next()
                        S.add("act", lambda e, ps=ps, kd=kd: e.copy(
                            out=kd.t[:, 0:256].rearrange("p (a r c) -> p a r c", a=2, r=2),
                            in_=ps.t[:, 0:128].rearrange("p (a r c) -> p a r c", a=2, r=1).broadcast_to([128, 2, 2, 64])),
                            reads=[ps.b], writes=[kd.b])
                        vx = Vx[gb % NBP]
                        S.add("act", lambda e, ps=ps, vx=vx: e.copy(out=vx.t[:, :, 0:64], in_=ps.t[:, 128:256].rearrange("p (a c) -> p a c", a=2)),
                              reads=[ps.b], writes=[vx.b])
                        vc = 1544 if n == 0 else 1545
                        S.add("dve", lambda e, vx=vx, vc=vc: e.tensor_copy(out=vx.t[:, :, 64:65],
                                                                           in_=cf.t[:, vc:vc + 1].unsqueeze(1).broadcast_to([128, 2, 1])),
                              reads=[cf.b], writes=[vx.b])
                        kt = kTd[gb % NBP]
                        transposes(kd, [kd.t[:, c * 128:(c + 1) * 128] for c in range(2)], kt,
                                   kt.t[:].rearrange("p a b -> p (a b)"), evac_eng="dve", delay=DL[1])
                    else:
                        gate_evac(ps, arena[j][1])

        att_pT = {}

        def seg_A2(l, k):
            for j, gb in enumerate(sbs[k]):
                s, n = blocks[gb]
                qT = arena[j][0]
                qv = qT.t.rearrange("p (a b) -> p a b", a=4)
                pTs = {}
                for kb in ((1,) if n == 0 else (0, 1)):
                    keyb = gb - 1 if kb == 0 else gb
                    kt = kTd[keyb % NBP]
                    scp = [PS(), PS()]

                    def fn(e, kt=kt, scp=scp, qv=qv):
                        for hh in range(2):
                            for kvh in range(2):
                                ins = e.matmul(scp[hh].t[:, kvh * 256:(kvh + 1) * 256].rearrange("p (a b) -> p a b", a=2),
                                               lhsT=kt.t[hh * 64:(hh + 1) * 64, kvh, :],
                                               rhs=qv[hh * 64:(hh + 1) * 64, 2 * kvh:2 * kvh + 2, :],
                                               start=True, stop=True)
                        return ins
                    S.add("pe", fn, reads=[kt.b, qT.b], writes=[scp[0].b, scp[1].b])
                    for hh in range(2):
                        ssb = f512.next()
                        S.add("dve", lambda e, ssb=ssb, hh=hh, kb=kb, scp=scp: e.tensor_tensor(
                            out=ssb.t[:], in0=scp[hh].t[:, :], in1=biasT[kb][hh].t[:], op=ALU.add),
                            reads=[scp[hh].b, biasT[kb][hh].b], writes=[ssb.b])
                        pT = pT_pool.next()
                        S.add("act", lambda e, ssb=ssb, pT=pT: e.activation(out=pT.t[:], in_=ssb.t[:], func=AF.Exp),
                              reads=[ssb.b], writes=[pT.b])
                        pTs[(kb, hh)] = pT
                att_pT[gb] = pTs
                srcs = [p.b for p in pTs.values()] + [Vx[gb % NBP].b, arena[j][1].b] + ([Vx[(gb - 1) % NBP].b] if n > 0 else [])
                defer(lambda l=l, j=j, gb=gb: att_out(l, j, gb), srcs, [brT_a[j]], DL[4])

        def att_out(l, j, gb):
            s, n = blocks[gb]
            pTs = att_pT.pop(gb)
            es_l = esink.t[:, l * 8:(l + 1) * 8]
            ops_ = [PS(), PS()]
            kbs = (1,) if n == 0 else (0, 1)

            def fn(e):
                for kvh in range(2):
                    for g in range(4):
                        cc, hh = g // 2, g % 2
                        for ki, kb in enumerate(kbs):
                            keyb = gb - 1 if kb == 0 else gb
                            ins = e.matmul(ops_[kvh].t[:, g * 65:(g + 1) * 65],
                                           lhsT=pTs[(kb, hh)].t[:, (kvh * 2 + cc) * 128:(kvh * 2 + cc + 1) * 128],
                                           rhs=Vx[keyb % NBP].t[:, kvh, :],
                                           start=(ki == 0), stop=(ki == len(kbs) - 1))
                return ins
            rd = [p.b for p in pTs.values()] + [Vx[gb % NBP].b] + ([Vx[(gb - 1) % NBP].b] if n > 0 else [])
            S.add("pe", fn, reads=rd, writes=[ops_[0].b, ops_[1].b])
            sm = small.next()
            ya = f512.next()
            for kvh in range(2):
                ov = ops_[kvh].t[:, 0:260].rearrange("p (g c) -> p g c", g=4)
                S.add("dve", lambda e, ov=ov, kvh=kvh: e.tensor_tensor(
                    out=sm.t[:, kvh * 4:(kvh + 1) * 4].unsqueeze(2), in0=ov[:, :, 64:65],
                    in1=es_l[:, kvh * 4:(kvh + 1) * 4].unsqueeze(2), op=ALU.add),
                    reads=[ops_[kvh].b, esink.b], writes=[sm.b])
            S.add("dve", lambda e: e.reciprocal(out=sm.t[:, 8:16], in_=sm.t[:, 0:8]), reads=[sm.b], writes=[sm.b])
            for kvh in range(2):
                ov = ops_[kvh].t[:, 0:260].rearrange("p (g c) -> p g c", g=4)
                S.add("dve", lambda e, ov=ov, kvh=kvh: e.tensor_tensor(
                    out=ya.t[:, kvh * 256:(kvh + 1) * 256].rearrange("p (g c) -> p g c", g=4), in0=ov[:, :, 0:64],
                    in1=sm.t[:, 8 + kvh * 4:8 + (kvh + 1) * 4].unsqueeze(2).broadcast_to([128, 4, 64]), op=ALU.mult),
                    reads=[ops_[kvh].b, sm.b], writes=[ya.b])
            yag = ybr.next()
            S.add("pool", lambda e: e.tensor_tensor(out=yag.t[:], in0=ya.t[:], in1=arena[j][1].t, op=ALU.mult),
                  reads=[ya.b, arena[j][1].b], writes=[yag.b])
            transposes(yag, [yag.t[:, c * 128:(c + 1) * 128] for c in range(4)], _V(None, brT_a[j]),
                       brT[j].t[:, 0:4, :].rearrange("p a b -> p (a b)"), delay=DL[3])

        def seg_A3(l, k):
            if PT_ALL:
                for j, gb in enumerate(sbs[k]):
                    att_out(l, j, gb)

        def seg_R1(l, k):
            K = l * NSB + k
            for nm in P2:
                wt = w_take()
                for j, gb in enumerate(sbs[k]):
                    s, n = blocks[gb]
                    ps = proj(wt, hTof(K, j), 512)
                    if nm in ("rq", "rk"):
                        rt = rt_tiles[j]
                        A = f512.next()
                        Bm = f512.next()
                        pv = ps.t[:, :].rearrange("p (h c) -> p h c", h=4)
                        S.add("dve", lambda e, A=A, pv=pv, rt=rt: e.tensor_tensor(
                            out=A.t[:].rearrange("p (h c) -> p h c", h=4), in0=pv,
                            in1=rt.t[:, 0:128].unsqueeze(1).broadcast_to([128, 4, 128]), op=ALU.mult),
                            reads=[ps.b, rt.b], writes=[A.b])
                        S.add("dve", lambda e, Bm=Bm, pv=pv, rt=rt: e.tensor_tensor(
                            out=Bm.t[:].rearrange("p (h c) -> p h c", h=4)[:, :, 0:64], in0=pv[:, :, 64:128],
                            in1=rt.t[:, 128:192].unsqueeze(1).broadcast_to([128, 4, 64]), op=ALU.mult),
                            reads=[ps.b, rt.b], writes=[Bm.b])
                        S.add("dve", lambda e, Bm=Bm, pv=pv, rt=rt: e.tensor_tensor(
                            out=Bm.t[:].rearrange("p (h c) -> p h c", h=4)[:, :, 64:128], in0=pv[:, :, 0:64],
                            in1=rt.t[:, 192:256].unsqueeze(1).broadcast_to([128, 4, 64]), op=ALU.mult),
                            reads=[ps.b, rt.b], writes=[Bm.b])
                        rot_sb = b512.next()
                        S.add("pool", lambda e, A=A, Bm=Bm, rot_sb=rot_sb: e.tensor_tensor(out=rot_sb.t[:], in0=A.t[:], in1=Bm.t[:], op=ALU.add),
                              reads=[A.b, Bm.b], writes=[rot_sb.b])
                        slot = 2 if nm == "rq" else 3
                        transposes(rot_sb, [rot_sb.t[:, c * 128:(c + 1) * 128] for c in range(4)], arena[j][slot],
                                   arena[j][slot].t, evac_mul=(gqT if nm == "rq" else gkT), delay=DL[2])
                        if nm == "rk":
                            S.add("pool", lambda e, rot_sb=rot_sb, j=j: e.tensor_tensor(
                                out=arena[j][4].t.rearrange("p (h c) -> p h c", h=4),
                                in0=rot_sb.t[:].rearrange("p (h c) -> p h c", h=4),
                                in1=gk_tok.unsqueeze(2).broadcast_to([128, 4, 128]), op=ALU.mult),
                                reads=[rot_sb.b, cf.b], writes=[arena[j][4].b])
                    elif nm == "rv":
                        S.add("act", lambda e, ps=ps, j=j: e.copy(out=arena[j][5].t, in_=ps.t[:, :]), reads=[ps.b], writes=[arena[j][5].b])
                    else:
                        gate_evac(ps, arena[j][6])

        ret_isT = {}

        def seg_R2(l, k):
            g128b = g128.unsqueeze(2).broadcast_to([128, 4, 128])
            for j, gb in enumerate(sbs[k]):
                s, n = blocks[gb]
                if n == nblk - 1:
                    continue
                ktok, vv = arena[j][4], arena[j][5]
                kvp = PS()

                def fn(e, kvp=kvp, ktok=ktok, vv=vv):
                    for hd in range(4):
                        sl = slice(hd * 128, (hd + 1) * 128)
                        ins = e.matmul(kvp.t[:, sl], lhsT=ktok.t[:, sl], rhs=vv.t[:, sl], start=True, stop=True)
                    return ins
                S.add("pe", fn, reads=[ktok.b, vv.b], writes=[kvp.b])
                if n == 0:
                    S.add("dve", lambda e, kvp=kvp: e.tensor_tensor(
                        out=Zst.t[:].rearrange("p (h c) -> p h c", h=4), in0=kvp.t[:, :].rearrange("p (h c) -> p h c", h=4),
                        in1=g128b, op=ALU.mult), reads=[kvp.b, cf.b], writes=[Zst.b])
                else:
                    tz = f512.next()
                    S.add("dve", lambda e, kvp=kvp, tz=tz: e.tensor_tensor(out=tz.t[:], in0=kvp.t[:, :], in1=Zst.t[:], op=ALU.add),
                          reads=[kvp.b, Zst.b], writes=[tz.b])
                    S.add("dve", lambda e, tz=tz: e.tensor_tensor(
                        out=Zst.t[:].rearrange("p (h c) -> p h c", h=4), in0=tz.t[:].rearrange("p (h c) -> p h c", h=4),
                        in1=g128b, op=ALU.mult), reads=[tz.b, cf.b], writes=[Zst.b])
                zbn = Zb[(gb + 1) % NBP]
                S.add("act", lambda e, zbn=zbn: e.copy(out=zbn.t[:], in_=Zst.t[:]), reads=[Zst.b], writes=[zbn.b])
            for j, gb in enumerate(sbs[k]):
                qrT, krT = arena[j][2], arena[j][3]
                isp = PS()

                def fn(e, isp=isp, qrT=qrT, krT=krT):
                    for hd in range(4):
                        ins = e.matmul(isp.t[:, hd * 128:(hd + 1) * 128], lhsT=krT.t[:, hd * 128:(hd + 1) * 128],
                                       rhs=qrT.t[:, hd * 128:(hd + 1) * 128], start=True, stop=True)
                    return ins
                S.add("pe", fn, reads=[qrT.b, krT.b], writes=[isp.b])
                isT = arena[j][4]
                S.add("dve", lambda e, isT=isT, isp=isp: e.tensor_tensor(out=isT.t, in0=isp.t[:, :], in1=maskT, op=ALU.mult),
                      reads=[isp.b, cf.b], writes=[isT.b])

        def seg_R3(l, k):
            for j, gb in enumerate(sbs[k]):
                s, n = blocks[gb]
                qrT, krT, isT, vv, sgr = arena[j][2:7]
                op_ = PS()
                zb = Zb[gb % NBP] if n > 0 else None

                def fn(e, op_=op_, isT=isT, vv=vv, qrT=qrT, zb=zb, n=n):
                    for hd in range(4):
                        sl = slice(hd * 128, (hd + 1) * 128)
                        ins = e.matmul(op_.t[:, sl], lhsT=isT.t[:, sl], rhs=vv.t[:, sl], start=True, stop=(n == 0))
                        if n > 0:
                            ins = e.matmul(op_.t[:, sl], lhsT=qrT.t[:, sl], rhs=zb.t[:, sl], start=False, stop=True)
                    return ins
                S.add("pe", fn, reads=[isT.b, vv.b, qrT.b] + ([zb.b] if n > 0 else []), writes=[op_.b])
                osb = f512.next()
                sq = f512.next()
                sm = small.next()
                tmp = small.next()
                S.add("act", lambda e, osb=osb, op_=op_: e.copy(out=osb.t[:], in_=op_.t[:, :]), reads=[op_.b], writes=[osb.b])
                S.add("act", lambda e, osb=osb, sq=sq: e.activation(out=sq.t[:], in_=osb.t[:], func=AF.Square), reads=[osb.b], writes=[sq.b])
                S.add("dve", lambda e, osb=osb, sm=sm: e.tensor_reduce(out=sm.t[:, 0:4], in_=osb.t[:].rearrange("p (h c) -> p h c", h=4),
                                                                       axis=AX.X, op=ALU.add), reads=[osb.b], writes=[sm.b])
                S.add("dve", lambda e, sq=sq, sm=sm: e.tensor_reduce(out=sm.t[:, 4:8], in_=sq.t[:].rearrange("p (h c) -> p h c", h=4),
                                                                     axis=AX.X, op=ALU.add), reads=[sq.b], writes=[sm.b])
                S.add("dve", lambda e, sm=sm: e.tensor_scalar(out=sm.t[:, 0:4], in0=sm.t[:, 0:4], scalar1=1.0 / 128, scalar2=None, op0=ALU.mult),
                      reads=[sm.b], writes=[sm.b])
                S.add("dve", lambda e, sm=sm: e.tensor_tensor(out=sm.t[:, 8:12], in0=sm.t[:, 0:4], in1=sm.t[:, 0:4], op=ALU.mult),
                      reads=[sm.b], writes=[sm.b])
                S.add("dve", lambda e, sm=sm: e.scalar_tensor_tensor(out=sm.t[:, 4:8], in0=sm.t[:, 4:8], scalar=1.0 / 128, in1=sm.t[:, 8:12],
                                                                     op0=ALU.mult, op1=ALU.subtract), reads=[sm.b], writes=[sm.b])
                S.add("dve", lambda e, sm=sm: e.tensor_scalar(out=sm.t[:, 4:8], in0=sm.t[:, 4:8], scalar1=GN_EPS, scalar2=None, op0=ALU.add),
                      reads=[sm.b], writes=[sm.b])
                rsqrt_small(sm.t[:, 12:16], sm.t[:, 4:8], tmp, 4, [sm.b])
                S.add("dve", lambda e, osb=osb, sm=sm: e.tensor_tensor(
                    out=osb.t[:].rearrange("p (h c) -> p h c", h=4), in0=osb.t[:].rearrange("p (h c) -> p h c", h=4),
                    in1=sm.t[:, 0:4].unsqueeze(2).broadcast_to([128, 4, 128]), op=ALU.subtract), reads=[osb.b, sm.b], writes=[osb.b])
                S.add("dve", lambda e, osb=osb, sm=sm: e.tensor_tensor(
                    out=osb.t[:].rearrange("p (h c) -> p h c", h=4), in0=osb.t[:].rearrange("p (h c) -> p h c", h=4),
                    in1=sm.t[:, 12:16].unsqueeze(2).broadcast_to([128, 4, 128]), op=ALU.mult), reads=[osb.b, sm.b], writes=[osb.b])
                yrg = ybr.next()
                S.add("pool", lambda e, osb=osb, yrg=yrg, sgr=sgr: e.tensor_tensor(out=yrg.t[:], in0=osb.t[:], in1=sgr.t, op=ALU.mult),
                      reads=[osb.b, sgr.b], writes=[yrg.b])
                transposes(yrg, [yrg.t[:, c * 128:(c + 1) * 128] for c in range(4)], _V(None, brT_r[j]),
                           brT[j].t[:, 4:8, :].rearrange("p a b -> p (a b)"), delay=DL[3])

        def seg_C1(l, k):
            K = l * NSB + k
            need_lc("conv", l)
            for nm in P3:
                wt = w_take()
                for j, gb in enumerate(sbs[k]):
                    ps = proj(wt, hTof(K, j), 512)
                    if nm == "cc":
                        c = ccf[j]
                        S.add("act", lambda e, c=c, ps=ps: e.copy(out=c.t[:], in_=ps.t[:, :]), reads=[ps.b], writes=[c.b])
                    elif nm == "cx":
                        c = ccf[j]
                        S.add("dve", lambda e, j=j, c=c, ps=ps: e.tensor_tensor(out=ust[j].t[:], in0=ps.t[:, :], in1=c.t[:], op=ALU.mult),
                              reads=[ps.b, c.b], writes=[ust[j].b])
                    elif nm == "cg":
                        gate_evac(ps, arena[j][0])
                    else:
                        S.add("dve", lambda e, ps=ps, j=j: e.tensor_tensor(out=arena[j][0].t, in0=ps.t[:, :], in1=arena[j][0].t, op=ALU.mult),
                              reads=[ps.b, arena[j][0].b], writes=[arena[j][0].b])

        def seg_C2(l, k):
            for j, gb in enumerate(sbs[k]):
                s_, n = blocks[gb]
                uc = uwt2[gb % 2]
                up = uwt2[(gb - 1) % 2]
                S.add("pool", lambda e, j=j, uc=uc: e.tensor_tensor(
                    out=uc.t[:], in0=ust[j].t[:].unsqueeze(1).broadcast_to([128, 3, 512]), in1=wconv.t[:], op=ALU.mult),
                    reads=[ust[j].b, wconv.b], writes=[uc.b])

                def emit(j=j, uc=uc, up=up, n=n):
                    yp = PS()

                    def fn(e):
                        e.matmul(yp.t[:, :], lhsT=cmat.t[:, 1, :], rhs=uc.t[:, 0, :], start=True, stop=False)
                        e.matmul(yp.t[:, :], lhsT=cmat.t[:, 2, :], rhs=uc.t[:, 1, :], start=False, stop=False)
                        ins = e.matmul(yp.t[:, :], lhsT=cmat.t[:, 0, :], rhs=uc.t[:, 2, :], start=False, stop=(n == 0))
                        if n > 0:
                            e.matmul(yp.t[:, :], lhsT=cmat.t[:, 3, :], rhs=up.t[:, 0, :], start=False, stop=False)
                            ins = e.matmul(yp.t[:, :], lhsT=cmat.t[:, 4, :], rhs=up.t[:, 1, :], start=False, stop=True)
                        return ins
                    S.add("pe", fn, reads=[uc.b, cmat.b] + ([up.b] if n > 0 else []), writes=[yp.b])
                    ycg = ybr.next()
                    S.add("dve", lambda e: e.tensor_tensor(out=ycg.t[:], in0=yp.t[:, :], in1=arena[j][0].t, op=ALU.mult),
                          reads=[yp.b, arena[j][0].b], writes=[ycg.b])
                    transposes(ycg, [ycg.t[:, c * 128:(c + 1) * 128] for c in range(4)], _V(None, brT_c[j]),
                               brT[j].t[:, 8:12, :].rearrange("p a b -> p (a b)"), delay=DL[3])
                defer(emit, [uc.b, up.b, arena[j][0].b], [brT_c[j]], DL[3])

        def seg_M(l, k, half, gs=(0, 1, 2)):
            K = l * NSB + k
            ab = act_blocks(l, k)
            hs = slice(half * 512, (half + 1) * 512)
            for g in gs:
                wm = w_take()
                tms = {}
                for j, gb in ab:
                    ps = proj(wm, hTof(K, j), 512)
                    tm = tm_pool.next()
                    S.add("act", lambda e, ps=ps, tm=tm: e.activation(out=tm.t[:], in_=ps.t[:, :], func=AF.Tanh, scale=0.5),
                          reads=[ps.b], writes=[tm.b])
                    tms[j] = tm
                wb = w_take()
                for j, gb in ab:
                    ps = proj(wb, _V(brT[j].t[:, 4 * g:4 * g + 4, :], (brT_a, brT_r, brT_c)[g][j]), 512, nkc=4)
                    tm = tms[j]
                    if g == 0:
                        S.add("dve", lambda e, ps=ps, tm=tm, j=j: e.scalar_tensor_tensor(
                            out=accmx[j].t[:, hs], in0=tm.t[:], scalar=1.0, in1=ps.t[:, :], op0=ALU.add, op1=ALU.mult),
                            reads=[tm.b, ps.b], writes=[accmx[j].b])
                    else:
                        pg = f512.next()
                        S.add("dve", lambda e, ps=ps, tm=tm, pg=pg: e.scalar_tensor_tensor(
                            out=pg.t[:], in0=tm.t[:], scalar=1.0, in1=ps.t[:, :], op0=ALU.add, op1=ALU.mult),
                            reads=[tm.b, ps.b], writes=[pg.b])
                        if g == 1:
                            S.add("pool", lambda e, pg=pg, j=j: e.tensor_tensor(
                                out=accmx[j].t[:, hs], in0=accmx[j].t[:, hs], in1=pg.t[:], op=ALU.add),
                                reads=[pg.b, accmx[j].b], writes=[accmx[j].b])
                        else:
                            S.add("pool", lambda e, pg=pg, j=j: e.tensor_tensor(
                                out=mp[j].t[:, hs], in0=accmx[j].t[:, hs], in1=pg.t[:], op=ALU.add),
                                reads=[pg.b, accmx[j].b], writes=[arena[j][2].b, arena[j][3].b])

        def seg_T(l, k):
            last = (l == depth - 1)
            ab = act_blocks(l, k)
            need_lc("post", l)
            for j, gb in ab:
                transposes(mp[j], [mp[j].t[:, c * 128:(c + 1) * 128] for c in range(8)], mpT[j],
                           mpT[j].t.rearrange("p a b -> p (a b)"), src_extra=[arena[j][3].b], dst_extra=[arena[j][5].b])
            for half in range(2):
                hs = slice(half * 512, (half + 1) * 512)
                wo = w_take()
                for j, gb in ab:
                    ps = proj(wo, mpT[j], 512, extra=[arena[j][5].b])
                    S.add("act", lambda e, ps=ps, j=j, hs=hs: e.copy(out=accmx[j].t[:, hs], in_=ps.t[:, :]), reads=[ps.b], writes=[accmx[j].b])
                    jk = b512.next()
                    S.add("act", lambda e, ps=ps, j=j, half=half, jk=jk: e.activation(
                        out=jk.t[:], in_=ps.t[:, :], func=AF.Square, accum_out=ssq[j].t[:, half:half + 1]),
                        reads=[ps.b], writes=[jk.b, ssq[j].b])

        tfin_x = {}

        def seg_TL(l, k, j):
            ab = dict(act_blocks(l, k))
            if j not in ab:
                return
            gb = ab[j]
            xr, xrsem = xr_pool.next_with_sem()
            tfin_x[(l, k, j)] = (xr, xrsem, xr_pool.i)
            load_x(xr, xrsem, l, gb)

        def seg_TC(l, k, j):
            last = (l == depth - 1)
            ab = dict(act_blocks(l, k))
            if j not in ab:
                return
            gb = ab[j]
            s_, n = blocks[gb]
            xr, xrsem, xi = tfin_x.pop((l, k, j))
            sq_ = ssq[j]
            S.add("dve", lambda e: e.scalar_tensor_tensor(out=sq_.t[:, 2:3], in0=sq_.t[:, 0:1], scalar=16.0 * 1024.0 * RMS_EPS,
                                                          in1=sq_.t[:, 1:2], op0=ALU.add, op1=ALU.add), reads=[sq_.b], writes=[sq_.b])
            rsqrt_small(sq_.t[:, 3:4], sq_.t[:, 2:3], small.next(), 1, [sq_.b])
            S.add("dve", lambda e: e.scalar_tensor_tensor(
                out=accmx[j].t[:], in0=accmx[j].t[:], scalar=sq_.t[:, 3:4], in1=gpost.t[:], op0=ALU.mult, op1=ALU.mult),
                reads=[accmx[j].b, sq_.b, gpost.b], writes=[accmx[j].b])
            S.add("dve", lambda e: e.tensor_tensor(out=xr.t[:], in0=accmx[j].t[:], in1=xr.t[:], op=ALU.add),
                  reads=[accmx[j].b, xr.b], writes=[xr.b])
            if last:
                S.add("sp", lambda e: e.dma_start(out=out_d[s_, (n - 1) * 128:n * 128, :], in_=xr.t[:]),
                      reads=[xr.b], writes=[out_bs[xi]], dma=xrsem)
            else:
                S.add("sp", lambda e: e.dma_start(out=xs_d[l][s_, n * 128:(n + 1) * 128, :], in_=xr.t[:]),
                      reads=[xr.b], writes=[xs_b[l][gb]], dma=xrsem)

        def tiles_of(seg, l, k, *a):
            if seg is seg_A1:
                return [win_tile(l, nm) for nm in P1]
            if seg is seg_R1:
                return [win_tile(l, nm) for nm in P2]
            if seg is seg_C1:
                return [win_tile(l, nm) for nm in P3]
            if seg is seg_M:
                half = a[0]
                gs = a[1] if len(a) > 1 else (0, 1, 2)
                r = []
                for g in gs:
                    r.append(win_tile(l, "m%d%d" % (g, half)))
                    r.append(wb_tile(l, g, half))
                return r
            if seg is seg_T:
                return [wo_tile(l, 0), wo_tile(l, 1)]
            return []

        sched = []
        allK = [(l, k) for l in range(depth) for k in range(NSB)]
        for i, (l, k) in enumerate(allK):
            nxt = allK[i + 1] if i + 1 < len(allK) else None
            prv = allK[i - 1] if i > 0 else None

            def TL(j):
                if prv and j < NB:
                    sched.append((seg_TL,) + prv + (j,))

            def TC(j):
                if prv and j < NB:
                    sched.append((seg_TC,) + prv + (j,))
            if i == 0:
                sched.append((seg_N, l, k))
                sched.append((seg_A1, l, k))
            def NL(j):
                if nxt:
                    sched.append((seg_NL,) + nxt + (j,))

            def NC(j):
                if nxt:
                    sched.append((seg_NC,) + nxt + (j,))
            sched.append((seg_A2, l, k))
            TL(0)
            TL(1)
            NL(0)
            NL(1)
            sched.append((seg_R1, l, k))
            TC(0)
            TL(2)
            NC(0)
            NL(2)
            sched.append((seg_C1, l, k))
            sched.append((seg_C2, l, k))
            NC(1)
            NL(3)
            TC(1)
            TL(3)
            sched.append((seg_R2, l, k))
            NC(2)
            TC(2)
            TC(3)
            NC(3)
            for j in range(4, NB):
                TL(j)
                TC(j)
                NL(j)
                NC(j)
            sched.append((seg_M, l, k, 0, (0,)))
            sched.append((seg_R3, l, k))
            sched.append((seg_M, l, k, 1, (0,)))
            if nxt:
                sched.append((seg_A1,) + nxt)
            sched.append((seg_M, l, k, 0, (1, 2)))
            sched.append((seg_M, l, k, 1, (1, 2)))
            sched.append((seg_T, l, k))
        for j in range(NB):
            sched.append((seg_TL,) + allK[-1] + (j,))
            sched.append((seg_TC,) + allK[-1] + (j,))
        for it in sched:
            wtiles.extend(tiles_of(*it))
        for it in sched:
            it[0](*it[1:])
        _flush(len(pending))

        SBUF_LEFT[0] = nc.sbuf_bytes_remaining
        S.add("sp", lambda e: e.wait_ge(sems["sp"], 0), reads=out_bs + [dbg_b])
        with nc.allow_non_contiguous_dma(reason="tiny constant loads"):
            with nc.Block() as block:
                S.emit(block, sems)
    return nc


_CACHE = {}
SBUF_LEFT = [0]


def _get_program(key):
    if key not in _CACHE:
        _CACHE[key] = build_program(*key)
    return _CACHE[key]


def kernel(x, meta_tokens, rel_bias, norm_pre, w_in, conv_w, attn_sinks, w_branch, w_out, norm_post):
    x = np.ascontiguousarray(np.asarray(x, dtype=np.float32))
    B, SL, _ = x.shape
    nblk = SL // 128 + 1
    depth = int(np.asarray(norm_pre).shape[0])
    nseq = B // N_CORES
    nc = _get_program((nseq, nblk, depth, 4, 3, None))
    rot, cf, cb = host_consts(nblk)
    shared = {
        "meta_tokens": np.ascontiguousarray(np.asarray(meta_tokens, np.float32)),
        "rel_bias": np.ascontiguousarray(np.asarray(rel_bias, np.float32)),
        "norm_pre": np.ascontiguousarray(np.asarray(norm_pre, np.float32)),
        "w_in": np.ascontiguousarray(np.asarray(w_in, np.float32)),
        "conv_w": np.ascontiguousarray(np.asarray(conv_w, np.float32)),
        "attn_sinks": np.ascontiguousarray(np.asarray(attn_sinks, np.float32)),
        "w_branch": np.ascontiguousarray(np.asarray(w_branch, np.float32)),
        "w_out": np.ascontiguousarray(np.asarray(w_out, np.float32)),
        "norm_post": np.ascontiguousarray(np.asarray(norm_post, np.float32)),
        "c_rot": rot, "c_f32": cf, "c_mat": cb,
    }
    in_maps = []
    for c in range(N_CORES):
        m = dict(shared)
        m["x"] = x[c * nseq:(c + 1) * nseq]
        in_maps.append(m)
    res = run_bass_kernel_spmd(nc, in_maps, core_ids=list(range(N_CORES)))
    return np.concatenate([np.asarray(r["out"]) for r in res.results], axis=0).astype(np.float32)
```

```python
import math
from contextlib import ExitStack

import numpy as np
import concourse.bass as bass
import concourse.mybir as mybir
from concourse.bass_utils import run_bass_kernel_spmd

F32 = mybir.dt.float32
BF16 = mybir.dt.bfloat16
I32 = mybir.dt.int32
ALU = mybir.AluOpType
AF = mybir.ActivationFunctionType
AX = mybir.AxisListType

D = 1024
PW = 8448
NEG = -30000.0
RMS_EPS = 1e-6
GN_EPS = 1e-6
N_CORES = 8

WIN_TILES = {
    "aq": (0, 512), "akv": (512, 768), "ag": (768, 1280),
    "rq": (1280, 1792), "rk": (1792, 2304), "rv": (2304, 2816), "rg": (2816, 3328),
    "cb": (3328, 3840), "cc": (3840, 4352), "cx": (4352, 4864), "cg": (4864, 5376),
}
for _g in range(3):
    for _h in range(2):
        WIN_TILES["m%d%d" % (_g, _h)] = (5376 + _g * 1024 + _h * 512, 5376 + _g * 1024 + (_h + 1) * 512)


class Buf:
    __slots__ = ("name", "writers", "readers")

    def __init__(self, name=""):
        self.name = name
        self.writers = {}
        self.readers = {}


class DmaSem:
    def __init__(self, sem, group=False):
        self.sem = sem
        self.count = 0
        self.group = group


class Op:
    __slots__ = ("eng", "fn", "waits", "signal", "dma", "order", "sigval")


ENGS = ("pe", "act", "dve", "pool", "sp")


class Sched:
    def __init__(self):
        self.q = {e: [] for e in ENGS}
        self.seen = {e: {} for e in ENGS}
        self.ncomp = {e: 0 for e in ENGS}

    def add(self, eng, fn, reads=(), writes=(), dma=None):
        op = Op()
        op.eng = eng
        op.fn = fn
        op.waits = []
        op.signal = False
        op.dma = dma
        op.sigval = None
        if dma is not None:
            dma.count += 16
            op.order = dma.count
        else:
            self.ncomp[eng] += 1
            op.order = self.ncomp[eng]
        deps = []
        for b in reads:
            deps.extend(b.writers.values())
        for b in writes:
            deps.extend(b.writers.values())
            deps.extend(b.readers.values())
        seen = self.seen[eng]
        for d in deps:
            if d is op:
                continue
            if d.dma is None and d.eng == "pe" and eng == "pe" and dma is None:
                continue
            if d.dma is not None and d.dma.group and d.dma is dma:
                continue
            key = id(d.dma) if d.dma is not None else d.eng
            order = (1 << 60) if (d.dma is not None and d.dma.group) else d.order
            if seen.get(key, -1) >= order:
                continue
            seen[key] = order
            op.waits.append(d)
            d.signal = True
        mykey = id(dma) if dma is not None else eng
        for b in writes:
            if b.readers:
                b.writers = {mykey: op}
                b.readers = {}
            else:
                b.writers[mykey] = op
        for b in reads:
            b.readers[mykey] = op
        self.q[eng].append(op)
        return op

    def emit(self, block, sems):
        for e in ENGS:
            c = 0
            for op in self.q[e]:
                if op.dma is None and op.signal:
                    c += 1
                    op.sigval = c
        decos = {"pe": block.tensor, "act": block.scalar, "dve": block.vector,
                 "pool": block.gpsimd, "sp": block.sync}
        for eng in ENGS:
            ops = self.q[eng]

            def body(e, ops=ops, eng=eng):
                for op in ops:
                    for d in op.waits:
                        if d.dma is not None:
                            e.wait_ge(d.dma.sem, d.dma.count if d.dma.group else d.order)
                        else:
                            e.wait_ge(sems[d.eng], d.sigval)
                    ins = op.fn(e)
                    if op.dma is not None:
                        ins.then_inc(op.dma.sem, 16)
                    elif op.signal:
                        ins.then_inc(sems[eng], 1)

            decos[eng](body)


class TB:
    __slots__ = ("t", "b")

    def __init__(self, t, name):
        self.t = t
        self.b = Buf(name)


def t5_bucket_np(d):
    d = np.asarray(d)
    n = np.maximum(d, 0)
    nf = np.maximum(n, 1).astype(np.float32)
    large = 16 + (np.log(nf / np.float32(16)) / np.float32(math.log(128 / 16)) * np.float32(16)).astype(np.int32)
    large = np.minimum(large, 31)
    return np.where(n < 16, n, large)


def host_consts(nblk):
    half = 64
    lin = np.linspace(0.0, 1.0, half, dtype=np.float32)
    theta = (np.float32(1.0) / np.power(np.float32(10000.0), lin)).astype(np.float32)
    pos = (np.arange(nblk * 128) - 112).astype(np.float32)
    ang = (pos[:, None] * theta[None, :]).astype(np.float32)
    cos = np.cos(ang).astype(np.float32)
    sin = np.sin(ang).astype(np.float32)
    rot = np.concatenate([cos, cos, -sin, sin], axis=1).reshape(nblk, 128, 256).astype(np.float32)

    lg = np.log1p(-(2.0 ** (-5.0 - np.arange(4, dtype=np.float64))))
    i = np.arange(128, dtype=np.float64)
    gq = np.exp(i[None, :] * lg[:, None])
    gk = np.exp(-i[None, :] * lg[:, None]) * (128.0 ** -0.5)
    g128 = np.exp(128.0 * lg)
    ncf = 1552
    cf = np.zeros((128, ncf), np.float32)
    jj = np.arange(128)[:, None]
    ii = np.arange(128)[None, :]
    mask = (ii >= jj).astype(np.float32)
    cf[:, 0:512] = np.tile(mask[:, None, :], (1, 4, 1)).reshape(128, 512)
    cf[:, 512:1024] = gq.reshape(1, 512)
    cf[:, 1024:1536] = gk.reshape(1, 512)
    cf[:, 1536:1540] = gk.T
    cf[:, 1540:1544] = g128[None, :]
    cf[:, 1544] = (np.arange(128) >= 112).astype(np.float32)
    cf[:, 1545] = 1.0
    cb = np.zeros((128, 5, 128), np.float32)
    tp = np.arange(128)[:, None]
    t = np.arange(128)[None, :]
    cb[:, 0, :] = (tp == t)
    cb[:, 1, :] = (tp == t - 2)
    cb[:, 2, :] = (tp == t - 1)
    cb[:, 3, :] = (tp == 128 + t - 2)
    cb[:, 4, :] = (tp == 128 + t - 1)
    return rot, cf, cb.reshape(128, 640)


def bucket_runs():
    bk = t5_bucket_np(np.arange(128))
    runs = []
    d = 16
    while d < 128:
        e = d
        while e + 1 < 128 and bk[e + 1] == bk[d]:
            e += 1
        runs.append((int(bk[d]), d, e))
        d = e + 1
    return runs


def build_program(nseq=2, nblk=17, depth=2, NB=4, NSLOT=3, dbg=None, PT_ALL=False, DEFER=True, DL=(6, 3, 6, 8, 6), NEWTON=2):
    nc = bass.Bass("TRN2", target_bir_lowering=False)
    S = Sched()
    SL = (nblk - 1) * 128
    P = 128

    def din(name, shape):
        return nc.dram_tensor(name, list(shape), F32, kind="ExternalInput").ap()

    x_d = din("x", [nseq, SL, D])
    meta_d = din("meta_tokens", [16, D])
    relb_d = din("rel_bias", [32, 8])
    npre_d = din("norm_pre", [depth, D])
    win_d = din("w_in", [depth, D, PW])
    convw_d = din("conv_w", [depth, 3, 512])
    sinks_d = din("attn_sinks", [depth, 8])
    wbr_d = din("w_branch", [depth, 3, 512, D])
    wout_d = din("w_out", [depth, D, D])
    npost_d = din("norm_post", [depth, D])
    rot_d = din("c_rot", [nblk, 128, 256])
    cf_d = din("c_f32", [128, 1552])
    cb_d = din("c_mat", [128, 640])
    out_d = nc.dram_tensor("out", [nseq, SL, D], F32, kind="ExternalOutput").ap()
    xs_h = [nc.dram_tensor("xs%d" % l, [nseq, nblk * 128, D], F32) for l in range(max(depth - 1, 1))]
    xs_d = [h.ap() for h in xs_h]
    xs_b = [[Buf("xs%d_%d" % (l, i)) for i in range(nseq * nblk)] for l in range(len(xs_h))]
    G_h = nc.dram_tensor("Gscr", [16, 128, 256], F32)
    G_b = Buf("G")

    es = ExitStack()
    with es:
        cnt = [0]

        def sb(shape, dt, name=None):
            cnt[0] += 1
            nm = "%s_%d" % (name or "t", cnt[0])
            return TB(es.enter_context(nc.sbuf_tensor(nm, list(shape), dt)), nm)

        def sem(name):
            cnt[0] += 1
            return es.enter_context(nc.semaphore("%s_%d" % (name, cnt[0])))

        def dsem(name, group=False):
            return DmaSem(sem(name), group)

        class Rot:
            def __init__(self, n, shape, dt, name, with_sem=False):
                self.items = [sb(shape, dt, name) for _ in range(n)]
                self.sems = [dsem("d_" + name) for _ in range(n)] if with_sem else None
                self.i = -1

            def next(self):
                self.i = (self.i + 1) % len(self.items)
                return self.items[self.i]

            def next_with_sem(self):
                self.i = (self.i + 1) % len(self.items)
                return self.items[self.i], self.sems[self.i]

        sems = {e: sem("s_" + e) for e in ENGS}

        psum = []
        for k in range(8):
            t = es.enter_context(nc.psum_tensor("ps%d" % k, [128, 512], F32))
            psum.append(TB(t, "ps%d" % k))
        ps_i = [-1]

        def PS():
            ps_i[0] = (ps_i[0] + 1) % 8
            return psum[ps_i[0]]

        cf = sb([128, 1552], F32, "cf")
        cmat = sb([128, 5, 128], BF16, "cmat")
        d_init = dsem("d_init", group=True)
        d_initp = dsem("d_initp", group=True)
        S.add("sp", lambda e: e.dma_start(out=cf.t[:], in_=cf_d), writes=[cf.b], dma=d_init)
        S.add("pool", lambda e: e.dma_start(out=cmat.t[:], in_=cb_d.rearrange("p (a b) -> p a b", a=5)),
              writes=[cmat.b], dma=d_initp)
        ident = cmat.t[:, 0, :]
        maskT = cf.t[:, 0:512]
        gqT = cf.t[:, 512:1024]
        gkT = cf.t[:, 1024:1536]
        gk_tok = cf.t[:, 1536:1540]
        g128 = cf.t[:, 1540:1544]

        esink = sb([128, depth * 8], F32, "esink")
        S.add("sp", lambda e: e.dma_start(
            out=esink.t[:], in_=sinks_d.rearrange("l h -> (l h)").rearrange("(o n) -> o n", o=1).broadcast_to([128, depth * 8])),
            writes=[esink.b], dma=d_init)
        S.add("act", lambda e: e.activation(out=esink.t[:], in_=esink.t[:], func=AF.Exp), reads=[esink.b], writes=[esink.b])

        rbT = sb([8, 32], F32, "rbT")
        Lt = sb([8, 384], F32, "Lt")
        biasT = [[sb([128, 512], BF16, "biasT") for _ in range(2)] for _ in range(2)]
        bias_stage = sb([128, 128], F32, "bstage")
        S.add("sp", lambda e: e.dma_start(out=rbT.t[:], in_=relb_d.rearrange("b h -> h b")), writes=[rbT.b], dma=d_init)
        S.add("dve", lambda e: e.memset(Lt.t[:], NEG), writes=[Lt.b])
        S.add("dve", lambda e: e.tensor_copy(out=Lt.t[:, 255:271], in_=rbT.t[:, 0:16]), reads=[rbT.b], writes=[Lt.b])
        S.add("dve", lambda e: e.tensor_copy(out=Lt.t[:, 0:15], in_=rbT.t[:, 1:16]), reads=[rbT.b], writes=[Lt.b])
        for (bk, dlo, dhi) in bucket_runs():
            n = dhi - dlo + 1
            S.add("dve", lambda e, bk=bk, dlo=dlo, n=n: e.tensor_copy(
                out=Lt.t[:, 255 + dlo:255 + dlo + n], in_=rbT.t[:, bk:bk + 1].broadcast_to([8, n])),
                reads=[rbT.b], writes=[Lt.b])
            S.add("dve", lambda e, bk=bk, dlo=dlo, n=n: e.tensor_copy(
                out=Lt.t[:, dlo - 1:dlo - 1 + n], in_=rbT.t[:, bk:bk + 1].broadcast_to([8, n])),
                reads=[rbT.b], writes=[Lt.b])
        d_G = dsem("d_G", group=True)
        d_bias = dsem("d_bias")
        for kb in range(2):
            S.add("sp", lambda e, kb=kb: e.dma_start(
                out=bass.AP(G_h, kb * 128 * 256, [[2 * 128 * 256, 8], [256, 128], [1, 256]]),
                in_=bass.AP(Lt.t, kb * 128, [[384, 8], [0, 128], [1, 256]])),
                reads=[Lt.b], writes=[G_b], dma=d_G)
        for h in range(8):
            kvh, g = h // 4, h % 4
            cc, hh = g // 2, g % 2
            for kb in range(2):
                dst = biasT[kb][hh]
                S.add("sp", lambda e, h=h, kb=kb: e.dma_start(
                    out=bias_stage.t[:],
                    in_=bass.AP(G_h, (h * 2 + kb) * 128 * 256 + 127, [[255, 128], [1, 128]])),
                    reads=[G_b], writes=[bias_stage.b], dma=d_bias)
                S.add("dve", lambda e, dst=dst, kvh=kvh, cc=cc: e.tensor_copy(
                    out=dst.t[:, (kvh * 2 + cc) * 128:(kvh * 2 + cc + 1) * 128], in_=bias_stage.t[:]),
                    reads=[bias_stage.b], writes=[dst.b])

        gpre = sb([128, D], F32, "gpre")
        gpost = sb([128, D], F32, "gpost")
        wconv = sb([128, 3, 512], F32, "wconv")
        d_lc = [dsem("d_lc") for _ in range(3)]

        def layer_consts(l):
            S.add("sp", lambda e: e.dma_start(out=gpre.t[:], in_=npre_d[l:l + 1, :].broadcast_to([128, D])),
                  writes=[gpre.b], dma=d_lc[0])
            S.add("sp", lambda e: e.dma_start(out=gpost.t[:], in_=npost_d[l:l + 1, :].broadcast_to([128, D])),
                  writes=[gpost.b], dma=d_lc[1])
            S.add("sp", lambda e: e.dma_start(
                out=wconv.t[:].rearrange("p a b -> p (a b)"),
                in_=convw_d[l].rearrange("k c -> (k c)").rearrange("(o n) -> o n", o=1).broadcast_to([128, 1536])),
                writes=[wconv.b], dma=d_lc[2])
            S.add("dve", lambda e: e.tensor_scalar(out=gpre.t[:], in0=gpre.t[:], scalar1=32.0, scalar2=None, op0=ALU.mult),
                  reads=[gpre.b], writes=[gpre.b])
            S.add("dve", lambda e: e.tensor_scalar(out=gpost.t[:], in0=gpost.t[:], scalar1=32.0, scalar2=None, op0=ALU.mult),
                  reads=[gpost.b], writes=[gpost.b])

        ring = [sb([128, 8, 512], BF16, "ring") for _ in range(NSLOT)]
        ring_sem = [dsem("d_ring") for _ in range(NSLOT)]
        wtiles = []
        w_issued = [0]

        def w_issue_upto(k):
            while w_issued[0] <= k and w_issued[0] < len(wtiles):
                i = w_issued[0]
                src, nkc, ncols = wtiles[i]
                slot = i % NSLOT
                S.add("pool", lambda e, src=src, nkc=nkc, ncols=ncols, slot=slot: e.dma_start(
                    out=ring[slot].t[:, 0:nkc, 0:ncols], in_=src),
                    writes=[ring[slot].b], dma=ring_sem[slot])
                w_issued[0] += 1

        w_next = [0]

        def w_take():
            k = w_next[0]
            w_next[0] += 1
            w_issue_upto(k + NSLOT - 1)
            return ring[k % NSLOT]

        NBP = NB + 1
        xt_pool = Rot(2, [128, D], F32, "xt", with_sem=True)
        rt_tiles = [sb([128, 256], F32, "rt") for _ in range(NB)]
        rt_sems = [dsem("d_rt") for _ in range(NB)]
        ccf = [sb([128, 512], BF16, "ccf") for _ in range(NB)]
        xr_pool = Rot(2, [128, D], F32, "xr", with_sem=True)
        out_bs = [Buf("out0"), Buf("out1")]
        st_pool = Rot(4, [128, 16], F32, "st")
        h_pool = Rot(2, [128, D], BF16, "h")
        hT2 = [[sb([128, 8, 128], BF16, "hT") for _ in range(NB)] for _ in range(2)]
        brT = [sb([128, 12, 128], BF16, "brT") for _ in range(NB)]
        brT_a = [Buf("brTa") for _ in range(NB)]
        brT_r = [Buf("brTr") for _ in range(NB)]
        brT_c = [Buf("brTc") for _ in range(NB)]
        arena_t = [es.enter_context(nc.sbuf_tensor("arena%d" % j, [128, 7, 512], BF16)) for j in range(NB)]

        class _V:
            __slots__ = ("t", "b")

            def __init__(self, t, b):
                self.t = t
                self.b = b
        arena = [[_V(arena_t[j][:, k, :], Buf("ar%d_%d" % (j, k))) for k in range(7)] for j in range(NB)]
        mpT = [_V(arena_t[j][:, 4:6, :].rearrange("p a (c d) -> p (a c) d", d=128), arena[j][4].b) for j in range(NB)]
        mp = [_V(arena_t[j][:, 2:4, :].rearrange("p a c -> p (a c)"), arena[j][2].b) for j in range(NB)]
        kTd = [sb([128, 2, 128], BF16, "kTd") for _ in range(NBP)]
        Vx = [sb([128, 2, 65], BF16, "Vx") for _ in range(NBP)]
        ust = [sb([128, 512], BF16, "ust") for _ in range(NB)]
        uwt2 = [sb([128, 3, 512], BF16, "uwt") for _ in range(2)]
        tm_pool = Rot(NB + 1, [128, 512], BF16, "tm")
        accmx = [sb([128, D], F32, "accmx") for _ in range(NB)]
        ssq = [sb([128, 4], F32, "ssq") for _ in range(NB)]
        f512 = Rot(5, [128, 512], F32, "f512")
        b512 = Rot(6, [128, 512], BF16, "b512")
        ybr = Rot(NB + 1, [128, 512], BF16, "ybr")
        pT_pool = Rot(8, [128, 512], BF16, "pT")
        Zst = sb([128, 512], F32, "Z")
        Zb = [sb([128, 512], BF16, "Zb") for _ in range(NBP)]
        small = Rot(6, [128, 16], F32, "small")

        def dump(name, tb, ap, shape):
            o = nc.dram_tensor("dbg_" + name, list(shape), F32, kind="ExternalOutput").ap()
            stg = sb(shape, F32, "dbgstg")
            S.add("dve", lambda e: e.tensor_copy(out=stg.t[:], in_=ap), reads=[tb.b], writes=[stg.b])
            S.add("sp", lambda e: e.dma_start(out=o, in_=stg.t[:]), reads=[stg.b], writes=[dbg_b], dma=d_dbg)

        d_dbg = dsem("d_dbg", group=True)
        dbg_b = Buf("dbg")

        pending = []
        depth_ = [0]
        _raw_add = S.add

        def _emit_items(items):
            depth_[0] += 1
            for it in items:
                it["emit"]()
            depth_[0] -= 1

        def _flush(n):
            items = [pending.pop(0) for _ in range(min(n, len(pending)))]
            _emit_items(items)

        def _flush_sel(hits):
            need = set(hits)
            changed = True
            while changed:
                changed = False
                for i in sorted(need):
                    bi = pending[i]
                    allb = bi["src"] | bi["dst"]
                    for a in range(i):
                        if a in need:
                            continue
                        ai = pending[a]
                        if (ai["dst"] & allb) or (ai["src"] & bi["dst"]):
                            need.add(a)
                            changed = True
            items = [pending[i] for i in sorted(need)]
            for i in sorted(need, reverse=True):
                pending.pop(i)
            _emit_items(items)

        def _add(eng, fn, reads=(), writes=(), dma=None):
            if pending:
                wids = set(id(b) for b in writes)
                ids = wids | set(id(b) for b in reads)
                hits = [i for i, it in enumerate(pending) if (it["dst"] & ids) or (it["src"] & wids)]
                if hits:
                    _flush_sel(hits)
                if eng == "pe" and depth_[0] == 0:
                    for it in pending:
                        it["age"] += 1
                    n = 0
                    while n < len(pending) and pending[n]["age"] > pending[n]["delay"]:
                        n += 1
                    if n:
                        _flush(n)
            return _raw_add(eng, fn, reads, writes, dma)
        S.add = _add

        def defer(emit, src, dst, delay):
            if delay <= 0 or not DEFER:
                emit()
                return
            pending.append({"emit": emit, "src": set(id(b) for b in src), "dst": set(id(b) for b in dst), "age": 0, "delay": delay})

        def rsqrt_small(dst_ap, src_ap, tmp, ncol, srcb):
            ti = tmp.t.bitcast(I32)
            a = tmp.t[:, 0:ncol]
            b = tmp.t[:, ncol:2 * ncol]
            c = tmp.t[:, 2 * ncol:3 * ncol]
            rw = list(srcb) + [tmp.b]
            S.add("dve", lambda e: e.tensor_single_scalar(out=ti[:, 0:ncol], in_=src_ap.bitcast(I32), scalar=1,
                                                          op=ALU.arith_shift_right), reads=rw, writes=[tmp.b])
            S.add("dve", lambda e: e.tensor_scalar(out=dst_ap.bitcast(I32), in0=ti[:, 0:ncol], scalar1=-1.0,
                                                   scalar2=1597463007.0, op0=ALU.mult, op1=ALU.add), reads=rw, writes=rw)
            for _ in range(NEWTON):
                if ncol == 1:
                    S.add("dve", lambda e: e.scalar_tensor_tensor(out=b, in0=dst_ap, scalar=src_ap, in1=dst_ap, op0=ALU.mult, op1=ALU.mult),
                          reads=rw, writes=rw)
                else:
                    S.add("dve", lambda e: e.tensor_tensor(out=a, in0=dst_ap, in1=dst_ap, op=ALU.mult), reads=rw, writes=rw)
                    S.add("dve", lambda e: e.tensor_tensor(out=b, in0=a, in1=src_ap, op=ALU.mult), reads=rw, writes=rw)
                S.add("dve", lambda e: e.tensor_scalar(out=c, in0=b, scalar1=-0.5, scalar2=1.5, op0=ALU.mult, op1=ALU.add),
                      reads=rw, writes=rw)
                S.add("dve", lambda e: e.tensor_tensor(out=dst_ap, in0=dst_ap, in1=c, op=ALU.mult), reads=rw, writes=rw)

        def transposes(src_tb, src_aps, dst_tb, dst_ap, evac_eng="act", evac_mul=None, src_extra=(), dst_extra=(), delay=0):
            n = len(src_aps)

            def emit():
                ps = PS()
                psb = ps.t.bitcast(BF16)

                def fn(e):
                    for k, ap in enumerate(src_aps):
                        ins = e.transpose(out=psb[:, k * 128:(k + 1) * 128], in_=ap, identity=ident)
                    return ins
                S.add("pe", fn, reads=[src_tb.b, cmat.b] + list(src_extra), writes=[ps.b])
                src = psb[:, 0:n * 128]
                if evac_mul is not None:
                    S.add("dve", lambda e: e.tensor_tensor(out=dst_ap, in0=src, in1=evac_mul, op=ALU.mult),
                          reads=[ps.b, cf.b], writes=[dst_tb.b] + list(dst_extra))
                elif evac_eng == "act":
                    S.add("act", lambda e: e.copy(out=dst_ap, in_=src), reads=[ps.b], writes=[dst_tb.b] + list(dst_extra))
                else:
                    S.add("dve", lambda e: e.tensor_copy(out=dst_ap, in_=src), reads=[ps.b], writes=[dst_tb.b] + list(dst_extra))
            defer(emit, [src_tb.b] + list(src_extra), [dst_tb.b] + list(dst_extra), delay)

        def gate_evac(ps, dst):
            t = f512.next()
            S.add("act", lambda e: e.activation(out=t.t[:], in_=ps.t[:, :], func=AF.Tanh, scale=0.5), reads=[ps.b], writes=[t.b])
            S.add("dve", lambda e: e.scalar_tensor_tensor(out=dst.t, in0=t.t[:], scalar=1.0, in1=ps.t[:, :],
                                                          op0=ALU.add, op1=ALU.mult), reads=[t.b, ps.b], writes=[dst.b])

        def proj(wt, lhs, ncols, nkc=8, extra=()):
            ps = PS()
            lhs_aps = [lhs.t[:, kc, :] for kc in range(nkc)]

            def fn(e):
                for kc in range(nkc):
                    ins = e.matmul(ps.t[:, 0:ncols], lhsT=lhs_aps[kc], rhs=wt.t[:, kc, 0:ncols],
                                   start=(kc == 0), stop=(kc == nkc - 1))
                return ins
            S.add("pe", fn, reads=[lhs.b, wt.b] + list(extra), writes=[ps.b])
            return ps

        def win_tile(l, nm):
            c0, c1 = WIN_TILES[nm]
            return (win_d[l, :, c0:c1].rearrange("(kc p) n -> p kc n", p=128), 8, c1 - c0)

        def wb_tile(l, g, half):
            return (wbr_d[l, g, :, half * 512:(half + 1) * 512].rearrange("(kc p) n -> p kc n", p=128), 4, 512)

        def wo_tile(l, half):
            return (wout_d[l, :, half * 512:(half + 1) * 512].rearrange("(kc p) n -> p kc n", p=128), 8, 512)

        blocks = [(s, n) for s in range(nseq) for n in range(nblk)]
        sbs = [list(range(i, min(i + NB, len(blocks)))) for i in range(0, len(blocks), NB)]
        NSB = len(sbs)
        P1 = ["aq", "akv", "ag"]
        P2 = ["rq", "rk", "rv", "rg"]
        P3 = ["cc", "cx", "cg", "cb"]
        lc_done = set()

        def need_lc(kind, l):
            if (kind, l) in lc_done:
                return
            lc_done.add((kind, l))
            if kind == "pre":
                S.add("sp", lambda e: e.dma_start(out=gpre.t[:], in_=npre_d[l:l + 1, :].broadcast_to([128, D])),
                      writes=[gpre.b], dma=d_lc[0])
                S.add("dve", lambda e: e.tensor_scalar(out=gpre.t[:], in0=gpre.t[:], scalar1=32.0, scalar2=None, op0=ALU.mult),
                      reads=[gpre.b], writes=[gpre.b])
            elif kind == "post":
                S.add("sp", lambda e: e.dma_start(out=gpost.t[:], in_=npost_d[l:l + 1, :].broadcast_to([128, D])),
                      writes=[gpost.b], dma=d_lc[1])
                S.add("dve", lambda e: e.tensor_scalar(out=gpost.t[:], in0=gpost.t[:], scalar1=32.0, scalar2=None, op0=ALU.mult),
                      reads=[gpost.b], writes=[gpost.b])
            else:
                S.add("sp", lambda e: e.dma_start(
                    out=wconv.t[:].rearrange("p a b -> p (a b)"),
                    in_=convw_d[l].rearrange("k c -> (k c)").rearrange("(o n) -> o n", o=1).broadcast_to([128, 1536])),
                    writes=[wconv.b], dma=d_lc[2])

        def hTof(K, j):
            return hT2[K % 2][j]

        def act_blocks(l, k):
            last = (l == depth - 1)
            return [(j, gb) for j, gb in enumerate(sbs[k]) if not (last and blocks[gb][1] == 0)]

        def load_x(dst, sem_, l, gb):
            s, n = blocks[gb]
            if l == 0:
                if n == 0:
                    S.add("dve", lambda e: e.memset(dst.t[:], 0.0), writes=[dst.b])
                    S.add("sp", lambda e: e.dma_start(out=dst.t[112:128, :], in_=meta_d), writes=[dst.b], dma=sem_)
                else:
                    S.add("sp", lambda e: e.dma_start(out=dst.t[:], in_=x_d[s, (n - 1) * 128:n * 128, :]), writes=[dst.b], dma=sem_)
            else:
                assert xs_b[l - 1][gb].writers, "schedule error: layer input read before the previous layer stored it"
                S.add("sp", lambda e: e.dma_start(out=dst.t[:], in_=xs_d[l - 1][s, n * 128:(n + 1) * 128, :]),
                      reads=[xs_b[l - 1][gb]], writes=[dst.b], dma=sem_)

        n_x = {}

        def seg_NL(l, k, j):
            if j >= len(sbs[k]):
                return
            gb = sbs[k][j]
            xt, xsem = xt_pool.next_with_sem()
            n_x[(l, k, j)] = xt
            load_x(xt, xsem, l, gb)

        def seg_NC(l, k, j):
            if j >= len(sbs[k]):
                return
            K = l * NSB + k
            need_lc("pre", l)
            gb = sbs[k][j]
            s_, n = blocks[gb]
            xt = n_x.pop((l, k, j))
            rt, rsem = rt_tiles[j], rt_sems[j]
            S.add("sp", lambda e: e.dma_start(out=rt.t[:], in_=rot_d[n]), writes=[rt.b], dma=rsem)
            st = st_pool.next()
            hh_ = h_pool.next()
            S.add("act", lambda e: e.activation(out=hh_.t[:], in_=xt.t[:], func=AF.Square, accum_out=st.t[:, 0:1]),
                  reads=[xt.b], writes=[hh_.b, st.b])
            S.add("dve", lambda e: e.tensor_scalar(out=st.t[:, 1:2], in0=st.t[:, 0:1], scalar1=1024.0 * RMS_EPS,
                                                   scalar2=None, op0=ALU.add), reads=[st.b], writes=[st.b])
            rsqrt_small(st.t[:, 2:3], st.t[:, 1:2], small.next(), 1, [st.b])
            S.add("dve", lambda e: e.scalar_tensor_tensor(
                out=hh_.t[:], in0=xt.t[:], scalar=st.t[:, 2:3], in1=gpre.t[:], op0=ALU.mult, op1=ALU.mult),
                reads=[xt.b, st.b, gpre.b], writes=[hh_.b])
            ht = hTof(K, j)
            transposes(hh_, [hh_.t[:, c * 128:(c + 1) * 128] for c in range(8)], ht, ht.t[:].rearrange("p a b -> p (a b)"), delay=DL[0])

        def seg_N(l, k):
            for j in range(len(sbs[k])):
                seg_NL(l, k, j)
                seg_NC(l, k, j)

        def seg_A1(l, k):
            K = l * NSB + k
            for nm in P1:
                wt = w_take()
                for j, gb in enumerate(sbs[k]):
                    s, n = blocks[gb]
                    c0, c1 = WIN_TILES[nm]
                    ps = proj(wt, hTof(K, j), c1 - c0)
                    if nm == "aq":
                        q_sb = b512.next()
                        S.add("act", lambda e, ps=ps, q_sb=q_sb: e.activation(out=q_sb.t[:], in_=ps.t[:, :], func=AF.Copy, scale=0.125),
                              reads=[ps.b], writes=[q_sb.b])
                        transposes(q_sb, [q_sb.t[:, c * 128:(c + 1) * 128] for c in range(4)], arena[j][0], arena[j][0].t, evac_eng="act", delay=DL[1])
                    elif nm == "akv":
                        kd = b512.next()
                        S.add("act", lambda e, ps=ps, kd=kd: e.copy(
                            out=kd.t[:, 0:256].rearrange("p (a r c) -> p a r c", a=2, r=2),
                            in_=ps.t[:, 0:128].rearrange("p (a r c) -> p a r c", a=2, r=1).broadcast_to([128, 2, 2, 64])),
                            reads=[ps.b], writes=[kd.b])
                        vx = Vx[gb % NBP]
                        S.add("act", lambda e, ps=ps, vx=vx: e.copy(out=vx.t[:, :, 0:64], in_=ps.t[:, 128:256].rearrange("p (a c) -> p a c", a=2)),
                              reads=[ps.b], writes=[vx.b])
                        vc = 1544 if n == 0 else 1545
                        S.add("dve", lambda e, vx=vx, vc=vc: e.tensor_copy(out=vx.t[:, :, 64:65],
                                                                           in_=cf.t[:, vc:vc + 1].unsqueeze(1).broadcast_to([128, 2, 1])),
                              reads=[cf.b], writes=[vx.b])
                        kt = kTd[gb % NBP]
                        transposes(kd, [kd.t[:, c * 128:(c + 1) * 128] for c in range(2)], kt,
                                   kt.t[:].rearrange("p a b -> p (a b)"), evac_eng="act", delay=DL[1])
                    else:
                        gate_evac(ps, arena[j][1])

        att_pT = {}

        def seg_A2(l, k):
            for j, gb in enumerate(sbs[k]):
                s, n = blocks[gb]
                qT = arena[j][0]
                qv = qT.t.rearrange("p (a b) -> p a b", a=4)
                pTs = {}
                for kb in ((1,) if n == 0 else (0, 1)):
                    keyb = gb - 1 if kb == 0 else gb
                    kt = kTd[keyb % NBP]
                    scp = [PS(), PS()]

                    def fn(e, kt=kt, scp=scp, qv=qv):
                        for hh in range(2):
                            for kvh in range(2):
                                ins = e.matmul(scp[hh].t[:, kvh * 256:(kvh + 1) * 256].rearrange("p (a b) -> p a b", a=2),
                                               lhsT=kt.t[hh * 64:(hh + 1) * 64, kvh, :],
                                               rhs=qv[hh * 64:(hh + 1) * 64, 2 * kvh:2 * kvh + 2, :],
                                               start=True, stop=True)
                        return ins
                    S.add("pe", fn, reads=[kt.b, qT.b], writes=[scp[0].b, scp[1].b])
                    for hh in range(2):
                        ssb = f512.next()
                        S.add("dve", lambda e, ssb=ssb, hh=hh, kb=kb, scp=scp: e.tensor_tensor(
                            out=ssb.t[:], in0=scp[hh].t[:, :], in1=biasT[kb][hh].t[:], op=ALU.add),
                            reads=[scp[hh].b, biasT[kb][hh].b], writes=[ssb.b])
                        pT = pT_pool.next()
                        S.add("act", lambda e, ssb=ssb, pT=pT: e.activation(out=pT.t[:], in_=ssb.t[:], func=AF.Exp),
                              reads=[ssb.b], writes=[pT.b])
                        pTs[(kb, hh)] = pT
                att_pT[gb] = pTs
                srcs = [p.b for p in pTs.values()] + [Vx[gb % NBP].b, arena[j][1].b] + ([Vx[(gb - 1) % NBP].b] if n > 0 else [])
                defer(lambda l=l, j=j, gb=gb: att_out(l, j, gb), srcs, [brT_a[j]], DL[4])

        def att_out(l, j, gb):
            s, n = blocks[gb]
            pTs = att_pT.pop(gb)
            es_l = esink.t[:, l * 8:(l + 1) * 8]
            ops_ = [PS(), PS()]
            kbs = (1,) if n == 0 else (0, 1)

            def fn(e):
                for kvh in range(2):
                    for g in range(4):
                        cc, hh = g // 2, g % 2
                        for ki, kb in enumerate(kbs):
                            keyb = gb - 1 if kb == 0 else gb
                            ins = e.matmul(ops_[kvh].t[:, g * 65:(g + 1) * 65],
                                           lhsT=pTs[(kb, hh)].t[:, (kvh * 2 + cc) * 128:(kvh * 2 + cc + 1) * 128],
                                           rhs=Vx[keyb % NBP].t[:, kvh, :],
                                           start=(ki == 0), stop=(ki == len(kbs) - 1))
                return ins
            rd = [p.b for p in pTs.values()] + [Vx[gb % NBP].b] + ([Vx[(gb - 1) % NBP].b] if n > 0 else [])
            S.add("pe", fn, reads=rd, writes=[ops_[0].b, ops_[1].b])
            sm = small.next()
            ya = f512.next()
            for kvh in range(2):
                ov = ops_[kvh].t[:, 0:260].rearrange("p (g c) -> p g c", g=4)
                S.add("dve", lambda e, ov=ov, kvh=kvh: e.tensor_tensor(
                    out=sm.t[:, kvh * 4:(kvh + 1) * 4].unsqueeze(2), in0=ov[:, :, 64:65],
                    in1=es_l[:, kvh * 4:(kvh + 1) * 4].unsqueeze(2), op=ALU.add),
                    reads=[ops_[kvh].b, esink.b], writes=[sm.b])
            S.add("dve", lambda e: e.reciprocal(out=sm.t[:, 8:16], in_=sm.t[:, 0:8]), reads=[sm.b], writes=[sm.b])
            for kvh in range(2):
                ov = ops_[kvh].t[:, 0:260].rearrange("p (g c) -> p g c", g=4)
                S.add("dve", lambda e, ov=ov, kvh=kvh: e.tensor_tensor(
                    out=ya.t[:, kvh * 256:(kvh + 1) * 256].rearrange("p (g c) -> p g c", g=4), in0=ov[:, :, 0:64],
                    in1=sm.t[:, 8 + kvh * 4:8 + (kvh + 1) * 4].unsqueeze(2).broadcast_to([128, 4, 64]), op=ALU.mult),
                    reads=[ops_[kvh].b, sm.b], writes=[ya.b])
            yag = ybr.next()
            S.add("pool", lambda e: e.tensor_tensor(out=yag.t[:], in0=ya.t[:], in1=arena[j][1].t, op=ALU.mult),
                  reads=[ya.b, arena[j][1].b], writes=[yag.b])
            transposes(yag, [yag.t[:, c * 128:(c + 1) * 128] for c in range(4)], _V(None, brT_a[j]),
                       brT[j].t[:, 0:4, :].rearrange("p a b -> p (a b)"), delay=DL[3])

        def seg_A3(l, k):
            if PT_ALL:
                for j, gb in enumerate(sbs[k]):
                    att_out(l, j, gb)

        def seg_R1(l, k):
            K = l * NSB + k
            for nm in P2:
                wt = w_take()
                for j, gb in enumerate(sbs[k]):
                    s, n = blocks[gb]
                    ps = proj(wt, hTof(K, j), 512)
                    if nm in ("rq", "rk"):
                        rt = rt_tiles[j]
                        A = f512.next()
                        Bm = f512.next()
                        pv = ps.t[:, :].rearrange("p (h c) -> p h c", h=4)
                        S.add("dve", lambda e, A=A, pv=pv, rt=rt: e.tensor_tensor(
                            out=A.t[:].rearrange("p (h c) -> p h c", h=4), in0=pv,
                            in1=rt.t[:, 0:128].unsqueeze(1).broadcast_to([128, 4, 128]), op=ALU.mult),
                            reads=[ps.b, rt.b], writes=[A.b])
                        S.add("dve", lambda e, Bm=Bm, pv=pv, rt=rt: e.tensor_tensor(
                            out=Bm.t[:].rearrange("p (h c) -> p h c", h=4)[:, :, 0:64], in0=pv[:, :, 64:128],
                            in1=rt.t[:, 128:192].unsqueeze(1).broadcast_to([128, 4, 64]), op=ALU.mult),
                            reads=[ps.b, rt.b], writes=[Bm.b])
                        S.add("dve", lambda e, Bm=Bm, pv=pv, rt=rt: e.tensor_tensor(
                            out=Bm.t[:].rearrange("p (h c) -> p h c", h=4)[:, :, 64:128], in0=pv[:, :, 0:64],
                            in1=rt.t[:, 192:256].unsqueeze(1).broadcast_to([128, 4, 64]), op=ALU.mult),
                            reads=[ps.b, rt.b], writes=[Bm.b])
                        rot_sb = b512.next()
                        S.add("pool", lambda e, A=A, Bm=Bm, rot_sb=rot_sb: e.tensor_tensor(out=rot_sb.t[:], in0=A.t[:], in1=Bm.t[:], op=ALU.add),
                              reads=[A.b, Bm.b], writes=[rot_sb.b])
                        slot = 2 if nm == "rq" else 3
                        transposes(rot_sb, [rot_sb.t[:, c * 128:(c + 1) * 128] for c in range(4)], arena[j][slot],
                                   arena[j][slot].t, evac_mul=(gqT if nm == "rq" else gkT), delay=DL[2])
                        if nm == "rk":
                            S.add("pool", lambda e, rot_sb=rot_sb, j=j: e.tensor_tensor(
                                out=arena[j][4].t.rearrange("p (h c) -> p h c", h=4),
                                in0=rot_sb.t[:].rearrange("p (h c) -> p h c", h=4),
                                in1=gk_tok.unsqueeze(2).broadcast_to([128, 4, 128]), op=ALU.mult),
                                reads=[rot_sb.b, cf.b], writes=[arena[j][4].b])
                    elif nm == "rv":
                        S.add("act", lambda e, ps=ps, j=j: e.copy(out=arena[j][5].t, in_=ps.t[:, :]), reads=[ps.b], writes=[arena[j][5].b])
                    else:
                        gate_evac(ps, arena[j][6])

        ret_isT = {}

        def seg_R2(l, k):
            g128b = g128.unsqueeze(2).broadcast_to([128, 4, 128])
            for j, gb in enumerate(sbs[k]):
                s, n = blocks[gb]
                if n == nblk - 1:
                    continue
                ktok, vv = arena[j][4], arena[j][5]
                kvp = PS()

                def fn(e, kvp=kvp, ktok=ktok, vv=vv):
                    for hd in range(4):
                        sl = slice(hd * 128, (hd + 1) * 128)
                        ins = e.matmul(kvp.t[:, sl], lhsT=ktok.t[:, sl], rhs=vv.t[:, sl], start=True, stop=True)
                    return ins
                S.add("pe", fn, reads=[ktok.b, vv.b], writes=[kvp.b])
                if n == 0:
                    S.add("dve", lambda e, kvp=kvp: e.tensor_tensor(
                        out=Zst.t[:].rearrange("p (h c) -> p h c", h=4), in0=kvp.t[:, :].rearrange("p (h c) -> p h c", h=4),
                        in1=g128b, op=ALU.mult), reads=[kvp.b, cf.b], writes=[Zst.b])
                else:
                    tz = f512.next()
                    S.add("dve", lambda e, kvp=kvp, tz=tz: e.tensor_tensor(out=tz.t[:], in0=kvp.t[:, :], in1=Zst.t[:], op=ALU.add),
                          reads=[kvp.b, Zst.b], writes=[tz.b])
                    S.add("dve", lambda e, tz=tz: e.tensor_tensor(
                        out=Zst.t[:].rearrange("p (h c) -> p h c", h=4), in0=tz.t[:].rearrange("p (h c) -> p h c", h=4),
                        in1=g128b, op=ALU.mult), reads=[tz.b, cf.b], writes=[Zst.b])
                zbn = Zb[(gb + 1) % NBP]
                S.add("act", lambda e, zbn=zbn: e.copy(out=zbn.t[:], in_=Zst.t[:]), reads=[Zst.b], writes=[zbn.b])
            for j, gb in enumerate(sbs[k]):
                qrT, krT = arena[j][2], arena[j][3]
                isp = PS()

                def fn(e, isp=isp, qrT=qrT, krT=krT):
                    for hd in range(4):
                        ins = e.matmul(isp.t[:, hd * 128:(hd + 1) * 128], lhsT=krT.t[:, hd * 128:(hd + 1) * 128],
                                       rhs=qrT.t[:, hd * 128:(hd + 1) * 128], start=True, stop=True)
                    return ins
                S.add("pe", fn, reads=[qrT.b, krT.b], writes=[isp.b])
                isT = arena[j][4]
                S.add("dve", lambda e, isT=isT, isp=isp: e.tensor_tensor(out=isT.t, in0=isp.t[:, :], in1=maskT, op=ALU.mult),
                      reads=[isp.b, cf.b], writes=[isT.b])

        def seg_R3(l, k):
            for j, gb in enumerate(sbs[k]):
                s, n = blocks[gb]
                qrT, krT, isT, vv, sgr = arena[j][2:7]
                op_ = PS()
                zb = Zb[gb % NBP] if n > 0 else None

                def fn(e, op_=op_, isT=isT, vv=vv, qrT=qrT, zb=zb, n=n):
                    for hd in range(4):
                        sl = slice(hd * 128, (hd + 1) * 128)
                        ins = e.matmul(op_.t[:, sl], lhsT=isT.t[:, sl], rhs=vv.t[:, sl], start=True, stop=(n == 0))
                        if n > 0:
                            ins = e.matmul(op_.t[:, sl], lhsT=qrT.t[:, sl], rhs=zb.t[:, sl], start=False, stop=True)
                    return ins
                S.add("pe", fn, reads=[isT.b, vv.b, qrT.b] + ([zb.b] if n > 0 else []), writes=[op_.b])
                osb = f512.next()
                sq = f512.next()
                sm = small.next()
                tmp = small.next()
                S.add("act", lambda e, osb=osb, op_=op_: e.copy(out=osb.t[:], in_=op_.t[:, :]), reads=[op_.b], writes=[osb.b])
                S.add("act", lambda e, osb=osb, sq=sq: e.activation(out=sq.t[:], in_=osb.t[:], func=AF.Square), reads=[osb.b], writes=[sq.b])
                S.add("dve", lambda e, osb=osb, sm=sm: e.tensor_reduce(out=sm.t[:, 0:4], in_=osb.t[:].rearrange("p (h c) -> p h c", h=4),
                                                                       axis=AX.X, op=ALU.add), reads=[osb.b], writes=[sm.b])
                S.add("dve", lambda e, sq=sq, sm=sm: e.tensor_reduce(out=sm.t[:, 4:8], in_=sq.t[:].rearrange("p (h c) -> p h c", h=4),
                                                                     axis=AX.X, op=ALU.add), reads=[sq.b], writes=[sm.b])
                S.add("dve", lambda e, sm=sm: e.tensor_scalar(out=sm.t[:, 0:4], in0=sm.t[:, 0:4], scalar1=1.0 / 128, scalar2=None, op0=ALU.mult),
                      reads=[sm.b], writes=[sm.b])
                S.add("dve", lambda e, sm=sm: e.tensor_tensor(out=sm.t[:, 8:12], in0=sm.t[:, 0:4], in1=sm.t[:, 0:4], op=ALU.mult),
                      reads=[sm.b], writes=[sm.b])
                S.add("dve", lambda e, sm=sm: e.scalar_tensor_tensor(out=sm.t[:, 4:8], in0=sm.t[:, 4:8], scalar=1.0 / 128, in1=sm.t[:, 8:12],
                                                                     op0=ALU.mult, op1=ALU.subtract), reads=[sm.b], writes=[sm.b])
                S.add("dve", lambda e, sm=sm: e.tensor_scalar(out=sm.t[:, 4:8], in0=sm.t[:, 4:8], scalar1=GN_EPS, scalar2=None, op0=ALU.add),
                      reads=[sm.b], writes=[sm.b])
                rsqrt_small(sm.t[:, 12:16], sm.t[:, 4:8], tmp, 4, [sm.b])
                S.add("dve", lambda e, osb=osb, sm=sm: e.tensor_tensor(
                    out=osb.t[:].rearrange("p (h c) -> p h c", h=4), in0=osb.t[:].rearrange("p (h c) -> p h c", h=4),
                    in1=sm.t[:, 0:4].unsqueeze(2).broadcast_to([128, 4, 128]), op=ALU.subtract), reads=[osb.b, sm.b], writes=[osb.b])
                S.add("dve", lambda e, osb=osb, sm=sm: e.tensor_tensor(
                    out=osb.t[:].rearrange("p (h c) -> p h c", h=4), in0=osb.t[:].rearrange("p (h c) -> p h c", h=4),
                    in1=sm.t[:, 12:16].unsqueeze(2).broadcast_to([128, 4, 128]), op=ALU.mult), reads=[osb.b, sm.b], writes=[osb.b])
                yrg = ybr.next()
                S.add("pool", lambda e, osb=osb, yrg=yrg, sgr=sgr: e.tensor_tensor(out=yrg.t[:], in0=osb.t[:], in1=sgr.t, op=ALU.mult),
                      reads=[osb.b, sgr.b], writes=[yrg.b])
                transposes(yrg, [yrg.t[:, c * 128:(c + 1) * 128] for c in range(4)], _V(None, brT_r[j]),
                           brT[j].t[:, 4:8, :].rearrange("p a b -> p (a b)"), delay=DL[3])

        def seg_C1(l, k):
            K = l * NSB + k
            need_lc("conv", l)
            for nm in P3:
                wt = w_take()
                for j, gb in enumerate(sbs[k]):
                    ps = proj(wt, hTof(K, j), 512)
                    if nm == "cc":
                        c = ccf[j]
                        S.add("act", lambda e, c=c, ps=ps: e.copy(out=c.t[:], in_=ps.t[:, :]), reads=[ps.b], writes=[c.b])
                    elif nm == "cx":
                        c = ccf[j]
                        S.add("dve", lambda e, j=j, c=c, ps=ps: e.tensor_tensor(out=ust[j].t[:], in0=ps.t[:, :], in1=c.t[:], op=ALU.mult),
                              reads=[ps.b, c.b], writes=[ust[j].b])
                    elif nm == "cg":
                        gate_evac(ps, arena[j][0])
                    else:
                        S.add("dve", lambda e, ps=ps, j=j: e.tensor_tensor(out=arena[j][0].t, in0=ps.t[:, :], in1=arena[j][0].t, op=ALU.mult),
                              reads=[ps.b, arena[j][0].b], writes=[arena[j][0].b])

        def seg_C2(l, k):
            for j, gb in enumerate(sbs[k]):
                s_, n = blocks[gb]
                uc = uwt2[gb % 2]
                up = uwt2[(gb - 1) % 2]
                S.add("pool", lambda e, j=j, uc=uc: e.tensor_tensor(
                    out=uc.t[:], in0=ust[j].t[:].unsqueeze(1).broadcast_to([128, 3, 512]), in1=wconv.t[:], op=ALU.mult),
                    reads=[ust[j].b, wconv.b], writes=[uc.b])

                def emit(j=j, uc=uc, up=up, n=n):
                    yp = PS()

                    def fn(e):
                        e.matmul(yp.t[:, :], lhsT=cmat.t[:, 1, :], rhs=uc.t[:, 0, :], start=True, stop=False)
                        e.matmul(yp.t[:, :], lhsT=cmat.t[:, 2, :], rhs=uc.t[:, 1, :], start=False, stop=False)
                        ins = e.matmul(yp.t[:, :], lhsT=cmat.t[:, 0, :], rhs=uc.t[:, 2, :], start=False, stop=(n == 0))
                        if n > 0:
                            e.matmul(yp.t[:, :], lhsT=cmat.t[:, 3, :], rhs=up.t[:, 0, :], start=False, stop=False)
                            ins = e.matmul(yp.t[:, :], lhsT=cmat.t[:, 4, :], rhs=up.t[:, 1, :], start=False, stop=True)
                        return ins
                    S.add("pe", fn, reads=[uc.b, cmat.b] + ([up.b] if n > 0 else []), writes=[yp.b])
                    ycg = ybr.next()
                    S.add("dve", lambda e: e.tensor_tensor(out=ycg.t[:], in0=yp.t[:, :], in1=arena[j][0].t, op=ALU.mult),
                          reads=[yp.b, arena[j][0].b], writes=[ycg.b])
                    transposes(ycg, [ycg.t[:, c * 128:(c + 1) * 128] for c in range(4)], _V(None, brT_c[j]),
                               brT[j].t[:, 8:12, :].rearrange("p a b -> p (a b)"), delay=DL[3])
                defer(emit, [uc.b, up.b, arena[j][0].b], [brT_c[j]], DL[3])

        def seg_M(l, k, half, gs=(0, 1, 2)):
            K = l * NSB + k
            ab = act_blocks(l, k)
            hs = slice(half * 512, (half + 1) * 512)
            for g in gs:
                wm = w_take()
                tms = {}
                for j, gb in ab:
                    ps = proj(wm, hTof(K, j), 512)
                    tm = tm_pool.next()
                    S.add("act", lambda e, ps=ps, tm=tm: e.activation(out=tm.t[:], in_=ps.t[:, :], func=AF.Tanh, scale=0.5),
                          reads=[ps.b], writes=[tm.b])
                    tms[j] = tm
                wb = w_take()
                for j, gb in ab:
                    ps = proj(wb, _V(brT[j].t[:, 4 * g:4 * g + 4, :], (brT_a, brT_r, brT_c)[g][j]), 512, nkc=4)
                    tm = tms[j]
                    if g == 0:
                        S.add("dve", lambda e, ps=ps, tm=tm, j=j: e.scalar_tensor_tensor(
                            out=accmx[j].t[:, hs], in0=tm.t[:], scalar=1.0, in1=ps.t[:, :], op0=ALU.add, op1=ALU.mult),
                            reads=[tm.b, ps.b], writes=[accmx[j].b])
                    else:
                        pg = f512.next()
                        S.add("dve", lambda e, ps=ps, tm=tm, pg=pg: e.scalar_tensor_tensor(
                            out=pg.t[:], in0=tm.t[:], scalar=1.0, in1=ps.t[:, :], op0=ALU.add, op1=ALU.mult),
                            reads=[tm.b, ps.b], writes=[pg.b])
                        if g == 1:
                            S.add("pool", lambda e, pg=pg, j=j: e.tensor_tensor(
                                out=accmx[j].t[:, hs], in0=accmx[j].t[:, hs], in1=pg.t[:], op=ALU.add),
                                reads=[pg.b, accmx[j].b], writes=[accmx[j].b])
                        else:
                            S.add("pool", lambda e, pg=pg, j=j: e.tensor_tensor(
                                out=mp[j].t[:, hs], in0=accmx[j].t[:, hs], in1=pg.t[:], op=ALU.add),
                                reads=[pg.b, accmx[j].b], writes=[arena[j][2].b, arena[j][3].b])

        def seg_T(l, k):
            last = (l == depth - 1)
            ab = act_blocks(l, k)
            need_lc("post", l)
            for j, gb in ab:
                transposes(mp[j], [mp[j].t[:, c * 128:(c + 1) * 128] for c in range(8)], mpT[j],
                           mpT[j].t.rearrange("p a b -> p (a b)"), src_extra=[arena[j][3].b], dst_extra=[arena[j][5].b])
            for half in range(2):
                hs = slice(half * 512, (half + 1) * 512)
                wo = w_take()
                for j, gb in ab:
                    ps = proj(wo, mpT[j], 512, extra=[arena[j][5].b])
                    S.add("act", lambda e, ps=ps, j=j, hs=hs: e.copy(out=accmx[j].t[:, hs], in_=ps.t[:, :]), reads=[ps.b], writes=[accmx[j].b])
                    jk = b512.next()
                    S.add("act", lambda e, ps=ps, j=j, half=half, jk=jk: e.activation(
                        out=jk.t[:], in_=ps.t[:, :], func=AF.Square, accum_out=ssq[j].t[:, half:half + 1]),
                        reads=[ps.b], writes=[jk.b, ssq[j].b])

        tfin_x = {}

        def seg_TL(l, k, j):
            ab = dict(act_blocks(l, k))
            if j not in ab:
                return
            gb = ab[j]
            xr, xrsem = xr_pool.next_with_sem()
            tfin_x[(l, k, j)] = (xr, xrsem, xr_pool.i)
            load_x(xr, xrsem, l, gb)

        def seg_TC(l, k, j):
            last = (l == depth - 1)
            ab = dict(act_blocks(l, k))
            if j not in ab:
                return
            gb = ab[j]
            s_, n = blocks[gb]
            xr, xrsem, xi = tfin_x.pop((l, k, j))
            sq_ = ssq[j]
            S.add("dve", lambda e: e.scalar_tensor_tensor(out=sq_.t[:, 2:3], in0=sq_.t[:, 0:1], scalar=16.0 * 1024.0 * RMS_EPS,
                                                          in1=sq_.t[:, 1:2], op0=ALU.add, op1=ALU.add), reads=[sq_.b], writes=[sq_.b])
            rsqrt_small(sq_.t[:, 3:4], sq_.t[:, 2:3], small.next(), 1, [sq_.b])
            S.add("dve", lambda e: e.scalar_tensor_tensor(
                out=accmx[j].t[:], in0=accmx[j].t[:], scalar=sq_.t[:, 3:4], in1=gpost.t[:], op0=ALU.mult, op1=ALU.mult),
                reads=[accmx[j].b, sq_.b, gpost.b], writes=[accmx[j].b])
            S.add("dve", lambda e: e.tensor_tensor(out=xr.t[:], in0=accmx[j].t[:], in1=xr.t[:], op=ALU.add),
                  reads=[accmx[j].b, xr.b], writes=[xr.b])
            if last:
                S.add("sp", lambda e: e.dma_start(out=out_d[s_, (n - 1) * 128:n * 128, :], in_=xr.t[:]),
                      reads=[xr.b], writes=[out_bs[xi]], dma=xrsem)
            else:
                S.add("sp", lambda e: e.dma_start(out=xs_d[l][s_, n * 128:(n + 1) * 128, :], in_=xr.t[:]),
                      reads=[xr.b], writes=[xs_b[l][gb]], dma=xrsem)

        def tiles_of(seg, l, k, *a):
            if seg is seg_A1:
                return [win_tile(l, nm) for nm in P1]
            if seg is seg_R1:
                return [win_tile(l, nm) for nm in P2]
            if seg is seg_C1:
                return [win_tile(l, nm) for nm in P3]
            if seg is seg_M:
                half = a[0]
                gs = a[1] if len(a) > 1 else (0, 1, 2)
                r = []
                for g in gs:
                    r.append(win_tile(l, "m%d%d" % (g, half)))
                    r.append(wb_tile(l, g, half))
                return r
            if seg is seg_T:
                return [wo_tile(l, 0), wo_tile(l, 1)]
            return []

        sched = []
        allK = [(l, k) for l in range(depth) for k in range(NSB)]
        for i, (l, k) in enumerate(allK):
            nxt = allK[i + 1] if i + 1 < len(allK) else None
            prv = allK[i - 1] if i > 0 else None

            def TL(j):
                if prv and j < NB:
                    sched.append((seg_TL,) + prv + (j,))

            def TC(j):
                if prv and j < NB:
                    sched.append((seg_TC,) + prv + (j,))
            if i == 0:
                sched.append((seg_N, l, k))
                sched.append((seg_A1, l, k))
            def NL(j):
                if nxt:
                    sched.append((seg_NL,) + nxt + (j,))

            def NC(j):
                if nxt:
                    sched.append((seg_NC,) + nxt + (j,))
            sched.append((seg_A2, l, k))
            TL(0)
            TL(1)
            NL(0)
            NL(1)
            sched.append((seg_R1, l, k))
            TC(0)
            TL(2)
            NC(0)
            NL(2)
            sched.append((seg_C1, l, k))
            sched.append((seg_C2, l, k))
            NC(1)
            NL(3)
            TC(1)
            TL(3)
            sched.append((seg_R2, l, k))
            NC(2)
            TC(2)
            TC(3)
            NC(3)
            for j in range(4, NB):
                TL(j)
                TC(j)
                NL(j)
                NC(j)
            sched.append((seg_M, l, k, 0, (0,)))
            sched.append((seg_R3, l, k))
            sched.append((seg_M, l, k, 1, (0,)))
            if nxt:
                sched.append((seg_A1,) + nxt)
            sched.append((seg_M, l, k, 0, (1, 2)))
            sched.append((seg_M, l, k, 1, (1, 2)))
            sched.append((seg_T, l, k))
        for j in range(NB):
            sched.append((seg_TL,) + allK[-1] + (j,))
            sched.append((seg_TC,) + allK[-1] + (j,))
        for it in sched:
            wtiles.extend(tiles_of(*it))
        for it in sched:
            it[0](*it[1:])
        _flush(len(pending))

        SBUF_LEFT[0] = nc.sbuf_bytes_remaining
        S.add("sp", lambda e: e.wait_ge(sems["sp"], 0), reads=out_bs + [dbg_b])
        with nc.allow_non_contiguous_dma(reason="tiny constant loads"):
            with nc.Block() as block:
                S.emit(block, sems)
    return nc


_CACHE = {}
SBUF_LEFT = [0]


def _get_program(key):
    if key not in _CACHE:
        _CACHE[key] = build_program(*key)
    return _CACHE[key]


def kernel(x, meta_tokens, rel_bias, norm_pre, w_in, conv_w, attn_sinks, w_branch, w_out, norm_post):
    x = np.ascontiguousarray(np.asarray(x, dtype=np.float32))
    B, SL, _ = x.shape
    nblk = SL // 128 + 1
    depth = int(np.asarray(norm_pre).shape[0])
    nseq = B // N_CORES
    nc = _get_program((nseq, nblk, depth, 4, 3, None))
    rot, cf, cb = host_consts(nblk)
    shared = {
        "meta_tokens": np.ascontiguousarray(np.asarray(meta_tokens, np.float32)),
        "rel_bias": np.ascontiguousarray(np.asarray(rel_bias, np.float32)),
        "norm_pre": np.ascontiguousarray(np.asarray(norm_pre, np.float32)),
        "w_in": np.ascontiguousarray(np.asarray(w_in, np.float32)),
        "conv_w": np.ascontiguousarray(np.asarray(conv_w, np.float32)),
        "attn_sinks": np.ascontiguousarray(np.asarray(attn_sinks, np.float32)),
        "w_branch": np.ascontiguousarray(np.asarray(w_branch, np.float32)),
        "w_out": np.ascontiguousarray(np.asarray(w_out, np.float32)),
        "norm_post": np.ascontiguousarray(np.asarray(norm_post, np.float32)),
        "c_rot": rot, "c_f32": cf, "c_mat": cb,
    }
    in_maps = []
    for c in range(N_CORES):
        m = dict(shared)
        m["x"] = x[c * nseq:(c + 1) * nseq]
        in_maps.append(m)
    res = run_bass_kernel_spmd(nc, in_maps, core_ids=list(range(N_CORES)))
    return np.concatenate([np.asarray(r["out"]) for r in res.results], axis=0).astype(np.float32)
```
